# Optimizing a Trainium2 kernel written in Bass

```python
import math
import jax, jax.numpy as jnp
from jax import lax
import numpy as np

D_MODEL = 1024
BATCH = 4
SEQ = 4096
DEPTH = 4

ROPE_THETA = 10000.0
QBLK = 128
MLA_HEADS = 8
MLA_NOPE = 64
MLA_ROPE = 32
MLA_V = 64
MLA_Q_RANK = 256
MLA_KV_RANK = 128
FOX_HEADS = 8
FOX_DIM = 64
SWA_HEADS = 16
SWA_KV_HEADS = 2
SWA_DIM = 64
WINDOW = 128

RMS_EPS = 1e-6
LN_EPS = 1e-5
ALPHA = (2 * DEPTH) ** 0.25
BETA = (8 * DEPTH) ** -0.25

EVEN_WIDTH = MLA_HEADS * MLA_V + FOX_HEADS * FOX_DIM
ODD_WIDTH = SWA_HEADS * SWA_DIM
EVEN_SIZES = (MLA_Q_RANK, MLA_KV_RANK, MLA_ROPE, FOX_HEADS * FOX_DIM,
              FOX_HEADS * FOX_DIM, FOX_HEADS * FOX_DIM, FOX_HEADS, EVEN_WIDTH)
EVEN_IN = sum(EVEN_SIZES)
EVEN_V_START = MLA_Q_RANK + MLA_KV_RANK + MLA_ROPE + 2 * FOX_HEADS * FOX_DIM
ODD_SIZES = (SWA_HEADS * SWA_DIM, SWA_KV_HEADS * SWA_DIM, SWA_KV_HEADS * SWA_DIM, ODD_WIDTH)
ODD_IN = sum(ODD_SIZES)
ODD_V_START = SWA_HEADS * SWA_DIM + SWA_KV_HEADS * SWA_DIM
N_EVEN = (DEPTH + 1) // 2
N_ODD = DEPTH // 2

kernel_name = "hybrid_mla_fox_swa_deepnorm"


def _split(h, sizes):
    cuts = [int(c) for c in np.cumsum(sizes)[:-1]]
    return jnp.split(h, cuts, axis=-1)


def _heads(t, n_heads):
    b, s, _ = t.shape
    return t.reshape(b, s, n_heads, -1).transpose(0, 2, 1, 3)


def _merge(t):
    b, h, s, d = t.shape
    return t.transpose(0, 2, 1, 3).reshape(b, s, h * d)


def rms_norm(t, g):
    tf = t.astype(jnp.float32)
    tf = tf * lax.rsqrt(jnp.mean(tf * tf, axis=-1, keepdims=True) + RMS_EPS)
    return (tf * g.astype(jnp.float32)).astype(t.dtype)


def layer_norm(t, g, b):
    tf = t.astype(jnp.float32)
    mu = jnp.mean(tf, axis=-1, keepdims=True)
    var = jnp.mean(jnp.square(tf - mu), axis=-1, keepdims=True)
    y = (tf - mu) * lax.rsqrt(var + LN_EPS) * g.astype(jnp.float32) + b.astype(jnp.float32)
    return y.astype(t.dtype)


def rope(t, pos):
    d = t.shape[-1]
    inv = ROPE_THETA ** (-jnp.arange(0, d, 2, dtype=jnp.float32) / d)
    ang = pos.astype(jnp.float32)[:, None] * inv[None, :]
    cos, sin = jnp.cos(ang), jnp.sin(ang)
    t1, t2 = jnp.split(t.astype(jnp.float32), 2, axis=-1)
    return jnp.concatenate([t1 * cos - t2 * sin, t2 * cos + t1 * sin], axis=-1).astype(t.dtype)


def causal_block_attention(q, k, v, scale, cum_logf=None):
    b, h, s, dk = q.shape
    dv = v.shape[-1]
    nb = s // QBLK
    qb = q.reshape(b, h, nb, QBLK, dk).transpose(2, 0, 1, 3, 4)
    kpos = jnp.arange(s)
    idx = jnp.arange(nb)

    def block(args):
        if cum_logf is None:
            q_i, i = args
        else:
            q_i, c_i, i = args
        sc = jnp.einsum('bhqd,bhkd->bhqk', q_i, k,
                        preferred_element_type=jnp.float32) * scale
        if cum_logf is not None:
            sc = sc + c_i[..., :, None] - cum_logf[..., None, :]
        qpos = i * QBLK + jnp.arange(QBLK)
        mask = kpos[None, :] <= qpos[:, None]
        sc = jnp.where(mask, sc, -jnp.inf)
        p = jax.nn.softmax(sc, axis=-1)
        return jnp.einsum('bhqk,bhkd->bhqd', p.astype(v.dtype), v)

    if cum_logf is None:
        xs = (qb, idx)
    else:
        cb = cum_logf.reshape(b, h, nb, QBLK).transpose(2, 0, 1, 3)
        xs = (qb, cb, idx)
    out = lax.map(block, xs)
    return out.transpose(1, 2, 0, 3, 4).reshape(b, h, s, dv)


def sliding_window_sink_attention(q, k, v, sinks):
    b, h, s, d = q.shape
    hkv = k.shape[1]
    g = h // hkv
    nb = s // QBLK
    scale = d ** -0.5
    qb = q.reshape(b, hkv, g, nb, QBLK, d).transpose(3, 0, 1, 2, 4, 5)
    pad = ((0, 0), (0, 0), (QBLK, 0), (0, 0))
    kp = jnp.pad(k, pad)
    vp = jnp.pad(v, pad)
    sink = sinks.astype(jnp.float32).reshape(1, hkv, g, 1, 1)

    def block(args):
        q_i, i = args
        start = i * QBLK
        k_i = lax.dynamic_slice_in_dim(kp, start, 2 * QBLK, axis=2)
        v_i = lax.dynamic_slice_in_dim(vp, start, 2 * QBLK, axis=2)
        sc = jnp.einsum('bkgqd,bkjd->bkgqj', q_i, k_i,
                        preferred_element_type=jnp.float32) * scale
        qpos = start + jnp.arange(QBLK)
        kpos = start - QBLK + jnp.arange(2 * QBLK)
        diff = qpos[:, None] - kpos[None, :]
        mask = (diff >= 0) & (diff < WINDOW) & (kpos[None, :] >= 0)
        sc = jnp.where(mask, sc, -jnp.inf)
        logits = jnp.concatenate(
            [sc, jnp.broadcast_to(sink, sc.shape[:-1] + (1,))], axis=-1)
        p = jax.nn.softmax(logits, axis=-1)[..., :-1]
        return jnp.einsum('bkgqj,bkjd->bkgqd', p.astype(v.dtype), v_i)

    out = lax.map(block, (qb, jnp.arange(nb)))
    return out.transpose(1, 2, 3, 0, 4, 5).reshape(b, h, s, d)


def even_mixer(x, w_in, q_norm, w_uq, kv_norm, w_ukv, b_f, w_out, pos):
    b, s, _ = x.shape
    h = x @ w_in
    cq, ckv, k_pe, fq, fk, fv, f_logit, gate = _split(h, EVEN_SIZES)
    q = (rms_norm(cq, q_norm) @ w_uq).reshape(b, s, MLA_HEADS, MLA_NOPE + MLA_ROPE)
    q = q.transpose(0, 2, 1, 3)
    q_nope, q_pe = q[..., :MLA_NOPE], rope(q[..., MLA_NOPE:], pos)
    kv = (rms_norm(ckv, kv_norm) @ w_ukv).reshape(b, s, MLA_HEADS, MLA_NOPE + MLA_V)
    kv = kv.transpose(0, 2, 1, 3)
    k_nope, v_mla = kv[..., :MLA_NOPE], kv[..., MLA_NOPE:]
    k_pe = rope(k_pe[:, None], pos)
    q_mla = jnp.concatenate([q_nope, q_pe], axis=-1)
    k_mla = jnp.concatenate(
        [k_nope, jnp.broadcast_to(k_pe, (b, MLA_HEADS, s, MLA_ROPE))], axis=-1)
    o_mla = causal_block_attention(q_mla, k_mla, v_mla, (MLA_NOPE + MLA_ROPE) ** -0.5)
    log_f = jax.nn.log_sigmoid((f_logit + b_f).astype(jnp.float32))
    cum = lax.cumsum(log_f, axis=1).transpose(0, 2, 1)
    o_fox = causal_block_attention(_heads(fq, FOX_HEADS), _heads(fk, FOX_HEADS),
                                   _heads(fv, FOX_HEADS), FOX_DIM ** -0.5, cum)
    o = jnp.concatenate([_merge(o_mla), _merge(o_fox)], axis=-1)
    return (o * jax.nn.silu(gate)) @ w_out


def odd_mixer(x, w_in, sinks, w_out, pos):
    h = x @ w_in
    q, k, v, gate = _split(h, ODD_SIZES)
    q = rope(_heads(q, SWA_HEADS), pos)
    k = rope(_heads(k, SWA_KV_HEADS), pos)
    v = _heads(v, SWA_KV_HEADS)
    o = _merge(sliding_window_sink_attention(q, k, v, sinks))
    return (o * jax.nn.silu(gate)) @ w_out


def setup_inputs(seed: int = 0) -> dict:
    key = jax.random.key(seed)
    ks = jax.random.split(key, 16)
    nrm = jax.random.normal
    even_in_scale = jnp.ones((EVEN_IN,), jnp.float32).at[
        EVEN_V_START:EVEN_V_START + FOX_HEADS * FOX_DIM].set(BETA)
    ukv_scale = jnp.tile(jnp.concatenate([jnp.ones((MLA_NOPE,), jnp.float32),
                                          jnp.full((MLA_V,), BETA, jnp.float32)]), MLA_HEADS)
    odd_in_scale = jnp.ones((ODD_IN,), jnp.float32).at[
        ODD_V_START:ODD_V_START + SWA_KV_HEADS * SWA_DIM].set(BETA)
    return {
        "x": nrm(ks[0], (BATCH, SEQ, D_MODEL), jnp.float32),
        "even_w_in": nrm(ks[1], (N_EVEN, D_MODEL, EVEN_IN), jnp.float32) * D_MODEL ** -0.5 * even_in_scale,
        "even_q_norm": 1.0 + 0.02 * nrm(ks[2], (N_EVEN, MLA_Q_RANK), jnp.float32),
        "even_w_uq": nrm(ks[3], (N_EVEN, MLA_Q_RANK, MLA_HEADS * (MLA_NOPE + MLA_ROPE)), jnp.float32) * MLA_Q_RANK ** -0.5,
        "even_kv_norm": 1.0 + 0.02 * nrm(ks[4], (N_EVEN, MLA_KV_RANK), jnp.float32),
        "even_w_ukv": nrm(ks[5], (N_EVEN, MLA_KV_RANK, MLA_HEADS * (MLA_NOPE + MLA_V)), jnp.float32) * MLA_KV_RANK ** -0.5 * ukv_scale,
        "even_b_f": jax.random.uniform(ks[6], (N_EVEN, FOX_HEADS), jnp.float32, 1.0, 6.0),
        "even_w_out": nrm(ks[7], (N_EVEN, EVEN_WIDTH, D_MODEL), jnp.float32) * EVEN_WIDTH ** -0.5 * BETA,
        "even_ln_g": 1.0 + 0.02 * nrm(ks[8], (N_EVEN, D_MODEL), jnp.float32),
        "even_ln_b": 0.02 * nrm(ks[9], (N_EVEN, D_MODEL), jnp.float32),
        "odd_w_in": nrm(ks[10], (N_ODD, D_MODEL, ODD_IN), jnp.float32) * D_MODEL ** -0.5 * odd_in_scale,
        "odd_sinks": 0.5 * nrm(ks[11], (N_ODD, SWA_HEADS), jnp.float32),
        "odd_w_out": nrm(ks[12], (N_ODD, ODD_WIDTH, D_MODEL), jnp.float32) * ODD_WIDTH ** -0.5 * BETA,
        "odd_ln_g": 1.0 + 0.02 * nrm(ks[13], (N_ODD, D_MODEL), jnp.float32),
        "odd_ln_b": 0.02 * nrm(ks[14], (N_ODD, D_MODEL), jnp.float32),
    }


def reference(x, even_w_in, even_q_norm, even_w_uq, even_kv_norm, even_w_ukv, even_b_f,
              even_w_out, even_ln_g, even_ln_b, odd_w_in, odd_sinks, odd_w_out,
              odd_ln_g, odd_ln_b):
    pos = jnp.arange(x.shape[1])
    for layer in range(DEPTH):
        j = layer // 2
        if layer % 2 == 0:
            y = even_mixer(x, even_w_in[j], even_q_norm[j], even_w_uq[j], even_kv_norm[j],
                           even_w_ukv[j], even_b_f[j], even_w_out[j], pos)
            x = layer_norm(ALPHA * x + y, even_ln_g[j], even_ln_b[j])
        else:
            y = odd_mixer(x, odd_w_in[j], odd_sinks[j], odd_w_out[j], pos)
            x = layer_norm(ALPHA * x + y, odd_ln_g[j], odd_ln_b[j])
    return x
```

```python
import contextlib
import numpy as np
import concourse.bass as bass
import concourse.mybir as mybir
from concourse.bass_utils import run_bass_kernel_spmd

F32 = mybir.dt.float32
BF16 = mybir.dt.bfloat16
AF = mybir.ActivationFunctionType
ALU = mybir.AluOpType

S = 4096
D = 1024
NB = 8
NT = 32
ALPHA = 8.0 ** 0.25
EVEN_IN = 2984
ODD_IN = 2304
N_CORES = 4

ENG_ATTR = {"pe": "tensor", "act": "scalar", "dve": "vector", "pool": "gpsimd", "sp": "sync"}
SEM_EPOCH = 30000
N_DMA_SEMS = 12
SAME_ENG_SYNC = ("dve", "act", "pool")


class Op:
    __slots__ = ("eng", "fn", "r", "w", "dma", "deps", "flag", "sig", "dsem", "dcnt", "bar")

    def __init__(self, eng, fn, r, w, dma, bar=False):
        self.eng = eng
        self.fn = fn
        self.r = r
        self.w = w
        self.dma = dma
        self.deps = ()
        self.flag = False
        self.sig = None
        self.dsem = None
        self.dcnt = None
        self.bar = bar


class Prog:
    def __init__(self, nc):
        self.nc = nc
        self.ops = []

    def add(self, eng, fn, r=(), w=(), dma=False, bar=False):
        self.ops.append(Op(eng, fn, tuple(r), tuple(w), dma, bar))

    def mm(self, out, lhsT, rhs, start, stop, r, w, **kw):
        self.add("pe", lambda e: e.matmul(out, lhsT=lhsT, rhs=rhs, start=start, stop=stop, **kw), r, w)

    def tr(self, out, in_, ident, r, w):
        self.add("pe", lambda e: e.transpose(out, in_, ident), r, w)

    def actv(self, out, in_, func, r, w, bias=None, scale=None):
        kw = {}
        if bias is not None:
            kw["bias"] = bias
        if scale is not None:
            kw["scale"] = scale
        self.add("act", lambda e: e.activation(out=out, in_=in_, func=func, **kw), r, w)

    def tt(self, eng, out, in0, in1, op, r, w):
        self.add(eng, lambda e: e.tensor_tensor(out=out, in0=in0, in1=in1, op=op), r, w)

    def ts(self, eng, out, in0, s1, s2, op0, op1, r, w):
        if op1 is None:
            self.add(eng, lambda e: e.tensor_scalar(out=out, in0=in0, scalar1=s1, scalar2=None, op0=op0), r, w)
        else:
            self.add(eng, lambda e: e.tensor_scalar(out=out, in0=in0, scalar1=s1, scalar2=s2, op0=op0, op1=op1), r, w)

    def stt(self, out, in0, scalar, in1, op0, op1, r, w):
        self.add("dve", lambda e: e.scalar_tensor_tensor(out=out, in0=in0, scalar=scalar, in1=in1, op0=op0, op1=op1), r, w)

    def cp(self, eng, out, in_, r, w):
        if eng == "act":
            self.add(eng, lambda e: e.copy(out=out, in_=in_), r, w)
        else:
            self.add(eng, lambda e: e.tensor_copy(out=out, in_=in_), r, w)

    def memset(self, eng, ap, val, w):
        self.add(eng, lambda e: e.memset(ap, val), (), w)

    def dma(self, out, in_, r, w, q="sp"):
        self.add(q, lambda e: e.dma_start(out=out, in_=in_), r, w, dma=True)

    def barrier(self, scratch_ap):
        self.add("dve", lambda e: e.memset(scratch_ap, 0.0), (), (), bar=True)

    def finalize(self, stack):
        nc = self.nc
        ops = self.ops
        last_w = {}
        readers = {}
        last_on_eng = {}
        dma_since_bar = []
        last_bar = None
        seen_after_bar = set()
        for i, op in enumerate(ops):
            deps = set()
            if op.bar:
                for e, j in last_on_eng.items():
                    deps.add(j)
                for j in dma_since_bar:
                    deps.add(j)
                dma_since_bar = []
                last_bar = i
                seen_after_bar = set()
            else:
                for k in op.r:
                    if k in last_w:
                        deps.add(last_w[k])
                for k in op.w:
                    if k in last_w:
                        deps.add(last_w[k])
                    for ri in readers.get(k, ()):
                        deps.add(ri)
                if last_bar is not None and op.eng not in seen_after_bar:
                    deps.add(last_bar)
                    seen_after_bar.add(op.eng)
            deps.discard(i)
            for k in op.r:
                readers.setdefault(k, []).append(i)
            for k in op.w:
                last_w[k] = i
                readers[k] = []
            if op.dma:
                dma_since_bar.append(i)
            else:
                last_on_eng[op.eng] = i
            need = []
            for d in deps:
                dop = ops[d]
                if dop.dma or op.dma or dop.eng != op.eng or op.eng in SAME_ENG_SYNC:
                    need.append(d)
                    if not dop.dma:
                        dop.flag = True
            op.deps = tuple(sorted(need))
        cnt = {}
        for op in ops:
            if op.dma:
                continue
            if op.flag:
                cnt[op.eng] = cnt.get(op.eng, 0) + 1
                op.sig = cnt[op.eng]
        engs = sorted({op.eng for op in ops})
        self.sems = {}
        for e in engs:
            n = cnt.get(e, 0) // SEM_EPOCH + 1
            self.sems[e] = [stack.enter_context(nc.semaphore(f"s_{e}_{k}")) for k in range(n)]
        self.dsems = [stack.enter_context(nc.semaphore(f"d_{k}")) for k in range(N_DMA_SEMS)]
        dtot = [0] * N_DMA_SEMS
        nd = 0
        for op in ops:
            if op.dma:
                s = nd % N_DMA_SEMS
                nd += 1
                dtot[s] += 16
                op.dsem = s
                op.dcnt = dtot[s]
        final_dma = list(dtot)
        block = stack.enter_context(nc.Block())
        by_eng = {e: [op for op in ops if op.eng == e] for e in engs}
        waited = {e: {} for e in engs}

        def emit_stream(ename, eh):
            wd = waited[ename]
            for op in by_eng[ename]:
                for d in op.deps:
                    dop = ops[d]
                    if dop.dma:
                        key = ("d", dop.dsem)
                        val = dop.dcnt
                        sem = self.dsems[dop.dsem]
                    else:
                        ep = (dop.sig - 1) // SEM_EPOCH
                        key = (dop.eng, ep)
                        val = dop.sig - ep * SEM_EPOCH
                        sem = self.sems[dop.eng][ep]
                    if wd.get(key, 0) >= val:
                        continue
                    wd[key] = val
                    eh.wait_ge(sem, val)
                if op.dma:
                    if op.dcnt > 16:
                        key = ("d", op.dsem)
                        if wd.get(key, 0) < op.dcnt - 16:
                            wd[key] = op.dcnt - 16
                            eh.wait_ge(self.dsems[op.dsem], op.dcnt - 16)
                    ins = op.fn(eh)
                    ins.then_inc(self.dsems[op.dsem], 16)
                else:
                    ins = op.fn(eh)
                    if op.flag:
                        ep = (op.sig - 1) // SEM_EPOCH
                        ins.then_inc(self.sems[op.eng][ep], 1)
            if ename == "sp":
                for s in range(N_DMA_SEMS):
                    if final_dma[s] > 0:
                        eh.wait_ge(self.dsems[s], final_dma[s])

        for ename in engs:
            deco = getattr(block, ENG_ATTR[ename])

            def body(eh, ename=ename):
                emit_stream(ename, eh)

            deco(body)
        return {e: len(by_eng[e]) for e in engs}


class Arena:
    def __init__(self, nc, nbytes):
        self.t = nc.alloc_sbuf_tensor("arena", [128, nbytes // 2], BF16)
        self.nbytes = nbytes

    def view(self, off, nbytes, dtype, pattern=None, **kw):
        assert off % 32 == 0 and off + nbytes <= self.nbytes, (off, nbytes)
        ap = self.t[:, off // 2:(off + nbytes) // 2]
        if dtype == F32:
            ap = ap.bitcast(F32)
        if pattern:
            ap = ap.rearrange(pattern, **kw)
        return ap


def build(nlayers=4, kinds='eoeo', debug=False, mla_heads=range(8), fox_heads=range(8), swa_heads=range(16)):
    nc = bass.Bass("TRN2", target_bir_lowering=False)
    dt_in = lambda n, s: nc.dram_tensor(n, s, F32, kind="ExternalInput").ap()
    x_in = dt_in("x", [S, D])
    ew_in = dt_in("even_w_in", [2, D, EVEN_IN])
    eqn = dt_in("even_q_norm", [2, 256])
    euq = dt_in("even_w_uq", [2, 256, 768])
    ekvn = dt_in("even_kv_norm", [2, 128])
    eukv = dt_in("even_w_ukv", [2, 128, 1024])
    ebf = dt_in("even_b_f", [2, 8])
    ewo = dt_in("even_w_out", [2, D, D])
    elg = dt_in("even_ln_g", [2, D])
    elb = dt_in("even_ln_b", [2, D])
    ow_in = dt_in("odd_w_in", [2, D, ODD_IN])
    osk = dt_in("odd_sinks", [2, 16])
    owo = dt_in("odd_w_out", [2, D, D])
    olg = dt_in("odd_ln_g", [2, D])
    olb = dt_in("odd_ln_b", [2, D])
    c_ident = dt_in("c_ident", [128, 128])
    c_mask4 = dt_in("c_mask4", [128, 2048])
    c_mswa = dt_in("c_mswa", [128, 256])
    c_r32c = dt_in("c_r32c", [32, S])
    c_r32s = dt_in("c_r32s", [32, S])
    c_r64c = dt_in("c_r64c", [64, S])
    c_r64s = dt_in("c_r64s", [64, S])
    c_sel = dt_in("c_sel", [8, 64 * 70])
    c_rot64 = dt_in("c_rot64", [128, 128])
    y_out = nc.dram_tensor("y", [S, D], F32, kind="ExternalOutput").ap()
    xres = nc.dram_tensor("xres", [S, D], F32).ap()
    ogT = (nc.dram_tensor("ogT", [D, S], BF16, kind="ExternalOutput") if debug else nc.dram_tensor("ogT", [D, S], BF16)).ap()

    if debug:
        dq = nc.dram_tensor("dQT", [128, S], BF16, kind="ExternalOutput").ap()
        dk_ = nc.dram_tensor("dKT", [128, S], BF16, kind="ExternalOutput").ap()
        dg = nc.dram_tensor("dGT", [128, S], BF16, kind="ExternalOutput").ap()
        dv = nc.dram_tensor("dVA", [128, 32 * 66], BF16, kind="ExternalOutput").ap()
        dlat = nc.dram_tensor("dlat", [128, 16384], BF16, kind="ExternalOutput").ap()
    P = Prog(nc)
    A = Arena(nc, 210944)
    o = 0
    xT = A.view(o, 65536, BF16, "p (c n) -> p c n", c=8); o += 65536
    ident = A.view(o, 512, F32); o += 512
    ones_f = A.view(o, 512, F32); o += 512
    mask4 = A.view(o, 4096, BF16); o += 4096
    mswa = A.view(o, 512, BF16); o += 512
    vec = A.view(o, 256, F32); o += 256
    ones_b = A.view(o, 256, BF16); o += 256
    identb = A.view(o, 256, BF16); o += 256
    rot64 = A.view(o, 512, F32); o += 512
    ones8 = A.view(o, 1024, BF16); o += 1024
    tmpf = [A.view(o + i * 2048, 2048, F32) for i in range(4)]; o += 8192
    wst = [A.view(o + i * 8192, 8192, F32, "p (c n) -> p c n", c=8) for i in range(2)]; o += 16384
    wb = A.view(o, 16384, BF16); o += 16384
    tC = [A.view(o + i * 2048, 2048, F32) for i in range(2)]; o += 4096
    tS = [A.view(o + i * 2048, 2048, F32) for i in range(2)]; o += 4096
    PH = o
    o = PH
    lat = A.view(o, 32768, BF16); o += 32768
    cqnT = lat[:, 0:8192].rearrange("p (c n) -> p c n", c=2)
    ckvnT = lat[:, 8192:12288]
    kpeT = lat[:, 12288:16384]
    csp = lat[:, 0:12288].rearrange("p (k n) -> p k n", k=3)
    lbuf = A.view(o, 16384, F32)
    cumf = A.view(o + 16384, 16384, F32)
    QT = A.view(o, 8192, BF16); o += 8192
    KT = A.view(o, 8192, BF16); o += 8192
    GT = A.view(o, 8192, BF16); o += 8192
    VA = A.view(o, 4224, BF16, "p (t d) -> p t d", t=32); o += 4224
    PT = [A.view(o + i * 1024, 1024, BF16) for i in range(4)]; o += 4096
    cqf = A.view(o, 4096, F32, "p (c n) -> p c n", c=2); o += 4096
    rr2 = [cqf[:, 0, :], cqf[:, 1, :]]
    rhi2 = [A.view(o + i * 1024, 1024, BF16) for i in range(2)]; o += 2048
    rlo2 = [A.view(o + i * 1024, 1024, BF16) for i in range(2)]; o += 2048
    ogb = [A.view(o + i * 1024, 1024, BF16) for i in range(2)]; o += 2048
    sel = A.view(o, 8960, BF16, "p (k m) -> p k m", k=64); o += 8960
    carry = A.view(o, 32, F32); o += 32
    assert o <= 210944, o
    o = PH
    wo = A.view(o, 16384, BF16, "p (c n) -> p c n", c=8); o += 16384
    gt = A.view(o, 4096, F32); o += 4096
    bt = A.view(o, 4096, F32); o += 4096
    ogt = [A.view(o + i * 8192, 8192, BF16, "p (c n) -> p c n", c=8) for i in range(2)]; o += 16384
    xin = [A.view(o + i * 4096, 4096, F32) for i in range(2)]; o += 8192
    zt2 = [A.view(o + i * 4096, 4096, F32) for i in range(2)]; o += 8192
    xn2 = [A.view(o + i * 4096, 4096, F32) for i in range(2)]; o += 8192
    xo = [A.view(o + i * 4096, 4096, F32) for i in range(3)]; o += 12288
    bst2 = [A.view(o + i * 64, 64, F32) for i in range(2)]; o += 128
    mv2 = [A.view(o + i * 32, 32, F32) for i in range(2)]; o += 64
    sd2 = [A.view(o + i * 32, 32, F32) for i in range(2)]; o += 64
    assert o <= 210944, o
    ps = [nc.alloc_psum_tensor(f"ps{i}", [128, 512], F32)[:, :] for i in range(8)]
    PS = lambda i: ("ps", i)

    wst_i = [0]

    def load_w(dst, src2d, C, n, dkey, neg=False):
        k = wst_i[0] % 2
        wst_i[0] += 1
        st = wst[k][:, 0:C, 0:n]
        P.dma(st, src2d.rearrange("(c p) n -> p c n", p=128), [], [("wst", k)])
        if neg:
            P.ts("pool", dst, st, -1.0, None, ALU.mult, None, [("wst", k)], list(dkey))
        else:
            P.cp("pool", dst, st, [("wst", k)], list(dkey))

    def wbv(off, C, n):
        return wb[:, off:off + C * n].rearrange("p (c n) -> p c n", c=C)

    P.dma(ident, c_ident, [], ["ident"])
    P.cp("dve", identb, ident, ["ident"], ["identb"])
    P.dma(rot64, c_rot64, [], ["rot64"])
    P.memset("dve", ones_f, 1.0, ["ones_f"])
    P.memset("dve", ones_b, 1.0, ["ones_b"])
    P.memset("dve", ones8, 1.0, ["ones8"])
    P.memset("dve", vec, 0.0, ["vec"])
    P.memset("dve", vec[:, 0:1], 1e-6, ["vec"])
    P.memset("dve", vec[:, 1:2], 1e-5, ["vec"])
    for i in range(4):
        P.dma(tmpf[0], c_mask4[:, i * 512:(i + 1) * 512], [], ["tmp0"])
        P.ts("dve", mask4[:, i * 512:(i + 1) * 512], tmpf[0], -1.0, 30000.0, ALU.add, ALU.mult, ["tmp0"], ["mask4"])
    P.dma(tmpf[1][:, 0:256], c_mswa, [], ["tmp1"])
    P.ts("dve", mswa, tmpf[1][:, 0:256], -1.0, 30000.0, ALU.add, ALU.mult, ["tmp1"], ["mswa"])

    def transpose_tile(src_ap, skey, t, evac=("act", "dve")):
        b0 = 2 + (t % 2) * 4
        for half in range(2):
            bank = b0 + half
            for cc in range(4):
                c = half * 4 + cc
                P.tr(ps[bank][:, cc * 128:(cc + 1) * 128], src_ap[:, c * 128:(c + 1) * 128], ident, [skey, "ident"], [PS(bank)])
            P.cp(evac[half], xT[:, half * 4:half * 4 + 4, t * 128:(t + 1) * 128],
                 ps[bank].rearrange("p (c n) -> p c n", c=4), [], [("xT", t // 4), PS(bank)])

    pbufs = [(xin[0], ("xin", 0)), (xin[1], ("xin", 1)), (xo[0], ("xo", 0)), (xo[1], ("xo", 1)), (xo[2], ("xo", 2)),
             (zt2[0], ("zt", 0))]
    NPB = len(pbufs)
    for t in range(min(NPB - 1, NT)):
        P.dma(pbufs[t % NPB][0], x_in[t * 128:(t + 1) * 128, :], [], [pbufs[t % NPB][1]])
    for t in range(NT):
        tn = t + NPB - 1
        if tn < NT:
            P.dma(pbufs[tn % NPB][0], x_in[tn * 128:(tn + 1) * 128, :], [], [pbufs[tn % NPB][1]])
        transpose_tile(pbufs[t % NPB][0], pbufs[t % NPB][1], t)

    P.barrier(vec[0:1, 60:61])

    def phase_a_init():
        P.memset("pool", QT, 0.0, [("QT", b) for b in range(NB)])
        P.memset("pool", KT, 0.0, [("KT", b) for b in range(NB)])
        P.memset("pool", VA, 1.0, ["VA"])

    def load_tables(b, cs, ss, nrow, dup=False):
        k = b % 2
        P.dma(tC[k][0:nrow, :], cs[:, b * 512:(b + 1) * 512], [], [("tC", k)])
        P.dma(tS[k][0:nrow, :], ss[:, b * 512:(b + 1) * 512], [], [("tS", k)])
        if dup:
            P.dma(tC[k][64:64 + nrow, :], cs[:, b * 512:(b + 1) * 512], [], [("tC", k)])
            P.dma(tS[k][64:64 + nrow, :], ss[:, b * 512:(b + 1) * 512], [], [("tS", k)])
        return k

    def rope_evac(dst, pm, pr, bm, br, k, nrow, dkey):
        P.tt("dve", tmpf[0][0:nrow, :], pm[0:nrow, :], tC[k][0:nrow, :], ALU.mult, [("tC", k)], ["tmp0", PS(bm)])
        P.tt("dve", tmpf[1][0:nrow, :], pr[0:nrow, :], tS[k][0:nrow, :], ALU.mult, [("tS", k)], ["tmp1", PS(br)])
        P.tt("pool", dst, tmpf[0][0:nrow, :], tmpf[1][0:nrow, :], ALU.add, ["tmp0", "tmp1"], [dkey])

    def proj_v(lhs_fn, rhs_fn, nk, rkeys):
        for g4 in range(4):
            bank = 6 + (g4 % 2)
            for tt_ in range(8):
                t = g4 * 8 + tt_
                for c in range(nk):
                    P.mm(ps[bank][:, tt_ * 64:(tt_ + 1) * 64], lhs_fn(c, t), rhs_fn(c), c == 0, c == nk - 1,
                         rkeys + [("xT", t // 4)], [PS(bank)], skip_group_check=True)
            P.cp("act" if g4 % 2 == 0 else "dve", VA[:, g4 * 8:(g4 + 1) * 8, 0:64],
                 ps[bank].rearrange("p (t d) -> p t d", t=8), [], ["VA", PS(bank)])

    def proj_gate(wg, wkey, M=64):
        for b in range(NB):
            bank = 6 + (b % 2)
            for c in range(8):
                P.mm(ps[bank][0:M, :], wg[:, c, :], xT[:, c, b * 512:(b + 1) * 512], c == 0, c == 7,
                     [*wkey, ("xT", b)], [PS(bank)])
            P.actv(GT[0:M, b * 512:(b + 1) * 512], ps[bank][0:M, :], AF.Silu, [], [("GT", b), PS(bank)])

    def gate_shift():
        allg = [("GT", b) for b in range(NB)]
        P.dma(GT[0:64, :], GT[64:128, :], allg, allg)

    def gate_mode(h, heads):
        heads = list(heads)
        if h % 2 == 0 and (h + 1) in heads:
            return "pair"
        if h % 2 == 1 and (h - 1) in heads:
            return "shift"
        return "single"

    def norm_prep(obank, j, sink_ap=None):
        ob = ps[obank]
        k = j % 2
        rr, rhi, rlo = rr2[k], rhi2[k], rlo2[k]
        ck = ("cqf", k)
        if sink_ap is not None:
            P.actv(rr[64:65, :], ob[64:65, :], AF.Ln, ["vec"], [ck, PS(obank)], bias=sink_ap)
        else:
            P.actv(rr[64:65, :], ob[64:65, :], AF.Ln, [], [ck, PS(obank)])
        P.actv(rr[64:65, :], rr[64:65, :], AF.Exp, [], [ck], scale=-1.0)
        P.cp("dve", rhi[64:65, :], rr[64:65, :], [ck], [("rhi", k)])
        P.tt("dve", rlo[64:65, :], rr[64:65, :], rhi[64:65, :], ALU.subtract, [ck, ("rhi", k)], [("rlo", k)])

    def norm_fin(obank, j, hrow, GTx=None, gname="GT"):
        ob = ps[obank]
        k = j % 2
        rhi, rlo = rhi2[k], rlo2[k]
        P.mm(ps[7][0:64, :], ones_b[64:65, 0:64], rhi[64:65, :], True, False, ["ones_b", ("rhi", k)], [PS(7)])
        P.mm(ps[7][0:64, :], ones_b[64:65, 0:64], rlo[64:65, :], False, True, ["ones_b", ("rlo", k)], [PS(7)])
        P.cp("act", tmpf[2][0:64, :], ps[7][0:64, :], [], ["tmp2", PS(7)])
        P.tt("dve", tmpf[3][0:64, :], ob[0:64, :], tmpf[2][0:64, :], ALU.mult, ["tmp2"], ["tmp3", PS(obank)])
        GTx = GT if GTx is None else GTx
        P.tt("pool", ogb[k][0:64, :], tmpf[3][0:64, :], GTx[0:64, j * 512:(j + 1) * 512], ALU.mult,
             ["tmp3", (gname, j)], [("ogb", k)])
        P.dma(ogT[hrow:hrow + 64, j * 512:(j + 1) * 512], ogb[k][0:64, :], [("ogb", k)], [("ogT", j)])

    def attn_causal(dk, scale, hrow):
        pairs = [(j, kb) for j in range(NB) for kb in range(4 * j + 4)]

        def c0_of(j, kb):
            return max(0, kb - 4 * j) * 128

        def qk_sm(i):
            j, kb = pairs[i]
            sb = i % 4
            diag = kb >= 4 * j
            c0 = c0_of(j, kb)
            P.mm(ps[sb][:, c0:512], KT[0:dk, kb * 128:(kb + 1) * 128], QT[0:dk, j * 512 + c0:(j + 1) * 512], True, not diag,
                 [("KT", kb // 4), ("QT", j)], [PS(sb)])
            if diag:
                r = kb - 4 * j
                P.mm(ps[sb][:, c0:512], identb, mask4[:, r * 512 + c0:(r + 1) * 512], False, True, ["identb", "mask4"], [PS(sb)])
            P.actv(PT[sb][:, c0:512], ps[sb][:, c0:512], AF.Exp, [], [("PT", sb), PS(sb)], scale=scale)

        def pv(i):
            j, kb = pairs[i]
            sb = i % 4
            obank = 4 + (j % 2)
            c0 = c0_of(j, kb)
            P.mm(ps[obank][0:65, c0:512], VA[:, kb, 0:65], PT[sb][:, c0:512], kb == 0, kb == 4 * j + 3,
                 ["VA", ("PT", sb)], [PS(obank)], skip_group_check=True)

        n = len(pairs)
        qk_sm(0)
        qk_sm(1)
        for i in range(n):
            if i + 2 < n:
                qk_sm(i + 2)
            pv(i)
            j, kb = pairs[i]
            if kb == 4 * j + 3:
                if j > 0:
                    norm_fin(4 + ((j - 1) % 2), j - 1, hrow)
                norm_prep(4 + (j % 2), j)
        norm_fin(4 + ((NB - 1) % 2), NB - 1, hrow)

    def attn_swa(hrow, sink_ap, QTx=None, qname="QT", GTx=None, gname="GT"):
        QTx = QT if QTx is None else QTx
        def obank_of(qb):
            return 4 + ((qb // 4) % 2)

        def qk(kb):
            n = 256 if kb < NT - 1 else 128
            sb = kb % 4
            P.mm(ps[sb][:, 0:n], KT[0:64, kb * 128:(kb + 1) * 128], QTx[0:64, kb * 128:kb * 128 + n], True, False,
                 [("KT", kb // 4), (qname, kb // 4), (qname, min((kb + 1) // 4, NB - 1))], [PS(sb)])
            P.mm(ps[sb][:, 0:n], identb, mswa[:, 0:n], False, True, ["identb", "mswa"], [PS(sb)])
            P.actv(PT[sb][:, 0:n], ps[sb][:, 0:n], AF.Exp, [], [("PT", sb), PS(sb)], scale=0.125)

        def pv(kb):
            sb = kb % 4
            ob = obank_of(kb)
            P.mm(ps[ob][0:65, (kb % 4) * 128:(kb % 4) * 128 + 128], VA[:, kb, 0:65], PT[sb][:, 0:128], kb == 0, True,
                 ["VA", ("PT", sb)], [PS(ob)], skip_group_check=True)
            if kb < NT - 1:
                ob2 = obank_of(kb + 1)
                c0 = ((kb + 1) % 4) * 128
                P.mm(ps[ob2][0:65, c0:c0 + 128], VA[:, kb, 0:65], PT[sb][:, 128:256], True, False,
                     ["VA", ("PT", sb)], [PS(ob2)], skip_group_check=True)
            if kb % 4 == 2 and kb // 4 > 0:
                norm_fin(obank_of(kb - 4), kb // 4 - 1, hrow, GTx, gname)
            if kb % 4 == 3:
                norm_prep(ob, kb // 4, sink_ap)

        qk(0)
        qk(1)
        for kb in range(NT):
            if kb + 2 < NT:
                qk(kb + 2)
            pv(kb)
        norm_fin(obank_of(NT - 1), NB - 1, hrow, GTx, gname)

    def stage_c(wo_src, g_src, b_src, x_src, x_dst, do_transpose):
        for q4 in range(4):
            load_w(wo[:, :, q4 * 256:(q4 + 1) * 256], wo_src[:, q4 * 256:(q4 + 1) * 256], 8, 256, ("wo",))
        P.dma(gt, g_src.partition_broadcast(128), [], ["gt"])
        P.dma(bt, b_src.partition_broadcast(128), [], ["bt"])
        ogv = ogT.rearrange("(c p) n -> p c n", p=128)

        def ld_og(g):
            P.dma(ogt[g % 2], ogv[:, :, g * 512:(g + 1) * 512], [("ogT", g)], [("ogt", g % 2)])

        def ld_x(t):
            P.dma(xin[t % 2], x_src[t * 128:(t + 1) * 128, :], [("xres", t)], [("xin", t % 2)])

        def finish(t):
            k3 = t % 3
            P.dma(x_dst[t * 128:(t + 1) * 128, :], xo[k3], [("xo", k3)], [("xres", t)], q="act")
            if do_transpose:
                transpose_tile(xo[k3], ("xo", k3), t, evac=("act", "act"))

        ld_og(0)
        ld_x(0)
        for t in range(NT):
            k = t % 2
            g, tl = t // 4, t % 4
            zt, xn, bst, mv, sd = zt2[k], xn2[k], bst2[k], mv2[k], sd2[k]
            kz, kx, kb_, km, ks = ("zt", k), ("xn", k), ("bst", k), ("mv", k), ("sd", k)
            if tl == 0 and g + 1 < NB:
                ld_og(g + 1)
            if t + 1 < NT:
                ld_x(t + 1)
            pb = (t % 2) * 4
            for half in range(2):
                for c in range(8):
                    P.mm(ps[pb + half], ogt[g % 2][:, c, tl * 128:(tl + 1) * 128], wo[:, c, half * 512:(half + 1) * 512], c == 0, c == 7,
                         [("ogt", g % 2), "wo"], [PS(pb + half)])
                P.stt(zt[:, half * 512:(half + 1) * 512], xin[k][:, half * 512:(half + 1) * 512], ALPHA, ps[pb + half],
                      ALU.mult, ALU.add, [("xin", k)], [kz, PS(pb + half)])
                P.add("dve", lambda e, half=half, bst=bst, zt=zt: e.bn_stats(out=bst[:, half * 6:(half + 1) * 6], in_=zt[:, half * 512:(half + 1) * 512]),
                      [kz], [kb_])
            P.add("dve", lambda e, bst=bst, mv=mv: e.bn_aggr(out=mv[:, 0:2], in_=bst[:, 0:12]), [kb_], [km])
            P.actv(sd[:, 0:1], mv[:, 1:2], AF.Sqrt, [km, "vec"], [ks], bias=vec[:, 1:2], scale=1.0)
            P.add("dve", lambda e, sd=sd: e.reciprocal(out=sd[:, 1:2], in_=sd[:, 0:1]), [], [ks])
            P.ts("dve", sd[:, 2:3], mv[:, 0:1], -1.0, sd[:, 1:2], ALU.mult, ALU.mult, [km], [ks])
            P.actv(xn, zt, AF.Identity, [kz, ks], [kx], bias=sd[:, 2:3], scale=sd[:, 1:2])
            if t >= 1:
                P.tt("dve", xo[(t - 1) % 3], xn2[1 - k], bt, ALU.add, [("xn", 1 - k), "bt"], [("xo", (t - 1) % 3)])
            if t >= 2:
                finish(t - 2)
            P.tt("pool", xn, xn, gt, ALU.mult, ["gt"], [kx])
        P.tt("dve", xo[(NT - 1) % 3], xn2[(NT - 1) % 2], bt, ALU.add, [("xn", (NT - 1) % 2), "bt"], [("xo", (NT - 1) % 3)])
        finish(NT - 2)
        finish(NT - 1)

    def run_heads(heads, wfn, projfn, attnfn):
        heads = list(heads)
        if not heads:
            return
        cur = wfn(heads[0])
        for i, h in enumerate(heads):
            projfn(h, cur)
            nxt = wfn(heads[i + 1]) if i + 1 < len(heads) else None
            attnfn(h)
            cur = nxt

    def even_layer(jl, x_src, x_dst, last):
        W = ew_in[jl]
        phase_a_init()
        wL = (("wb", 0), ("wb", 1))
        wfk = (("wb", 3),)
        wq0 = wbv(0, 8, 256)
        wkv0 = wbv(2048, 8, 128)
        wkp = wbv(3072, 8, 32)
        wkr = wbv(3328, 8, 32)
        load_w(wq0, W[:, 0:256], 8, 256, wL)
        load_w(wkv0, W[:, 256:384], 8, 128, wL)
        load_w(wkp, W[:, 384:416], 8, 32, wL)
        load_w(wkr[:, :, 0:16], W[:, 400:416], 8, 16, wL, neg=True)
        load_w(wkr[:, :, 16:32], W[:, 384:400], 8, 16, wL)
        for c in range(2):
            P.dma(vec[:, 2 + c:3 + c], eqn[jl, c * 128:(c + 1) * 128].rearrange("(p o) -> p o", o=1), [], ["vec"])
        P.dma(vec[:, 4:5], ekvn[jl, :].rearrange("(p o) -> p o", o=1), [], ["vec"])
        P.dma(vec[0:8, 5:6], ebf[jl, :].rearrange("(p o) -> p o", o=1), [], ["vec"])
        P.ts("pool", vec[0:8, 5:6], vec[0:8, 5:6], -1.0, None, ALU.mult, None, [], ["vec"])
        for b in range(NB):
            tk = load_tables(b, c_r32c, c_r32s, 32)
            xs = lambda c: xT[:, c, b * 512:(b + 1) * 512]
            for g, (wv_, bank) in enumerate([(wq0[:, :, 0:128], 0), (wq0[:, :, 128:256], 1), (wkv0, 2)]):
                for c in range(8):
                    P.mm(ps[bank], wv_[:, c, :], xs(c), c == 0, c == 7, [*wL, ("xT", b)], [PS(bank)])
            for (wv_, bank) in [(wkp, 3), (wkr, 6)]:
                for c in range(8):
                    P.mm(ps[bank][0:32, :], wv_[:, c, :], xs(c), c == 0, c == 7, [*wL, ("xT", b)], [PS(bank)])
            for c in range(2):
                P.cp("dve", cqf[:, c, :], ps[c], [], [("cqf", c), PS(c)])
                P.actv(tmpf[2 + c], cqf[:, c, :], AF.Square, [("cqf", c)], [f"tmp{2 + c}"])
            P.tt("dve", tmpf[2], tmpf[2], tmpf[3], ALU.add, ["tmp3"], ["tmp2"])
            P.mm(ps[7], ones_f, tmpf[2], True, True, ["ones_f", "tmp2"], [PS(7)])
            P.actv(tmpf[2], ps[7], AF.Sqrt, ["vec"], ["tmp2", PS(7)], bias=vec[:, 0:1], scale=1.0 / 256)
            P.add("dve", lambda e: e.reciprocal(out=tmpf[3], in_=tmpf[2]), ["tmp2"], ["tmp3"])
            for c in range(2):
                P.stt(cqnT[:, c, b * 512:(b + 1) * 512], cqf[:, c, :], vec[:, 2 + c:3 + c], tmpf[3], ALU.mult, ALU.mult,
                      [("cqf", c), "tmp3", "vec"], [("lat", b)])
            P.cp("dve", cqf[:, 0, :], ps[2], [], [("cqf", 0), PS(2)])
            P.actv(tmpf[2], cqf[:, 0, :], AF.Square, [("cqf", 0)], ["tmp2"])
            P.mm(ps[7], ones_f, tmpf[2], True, True, ["ones_f", "tmp2"], [PS(7)])
            P.actv(tmpf[2], ps[7], AF.Sqrt, ["vec"], ["tmp2", PS(7)], bias=vec[:, 0:1], scale=1.0 / 128)
            P.add("dve", lambda e: e.reciprocal(out=tmpf[3], in_=tmpf[2]), ["tmp2"], ["tmp3"])
            P.stt(ckvnT[:, b * 512:(b + 1) * 512], cqf[:, 0, :], vec[:, 4:5], tmpf[3], ALU.mult, ALU.mult,
                  [("cqf", 0), "tmp3", "vec"], [("lat", b)])
            rope_evac(kpeT[0:32, b * 512:(b + 1) * 512], ps[3], ps[6], 3, 6, tk, 32, ("lat", b))
        UQ = euq[jl]
        UKV = eukv[jl]
        for b in range(NB):
            P.cp("pool", KT[0:32, b * 512:(b + 1) * 512], kpeT[0:32, b * 512:(b + 1) * 512], [("lat", b)], [("KT", b)])

        def mla_w(h):
            wo_ = (h % 2) * 4096
            wq = wbv(wo_, 2, 128)
            wqr = wbv(wo_ + 256, 2, 32)
            wk = wb[:, wo_ + 320:wo_ + 448]
            wv = wb[:, wo_ + 448:wo_ + 512]
            gm = gate_mode(h, mla_heads)
            wg = wbv(wo_ + 512, 8, 128 if gm == "pair" else 64)
            wkey = (("wb", 2 * (h % 2)), ("wb", 2 * (h % 2) + 1))
            P.memset("pool", wq[:, :, 32:64], 0.0, list(wkey))
            P.memset("pool", wk[:, 0:64], 0.0, [*wkey])
            load_w(wq[:, :, 0:32], UQ[:, h * 96 + 64:h * 96 + 96], 2, 32, wkey)
            load_w(wq[:, :, 64:128], UQ[:, h * 96:h * 96 + 64], 2, 64, wkey)
            load_w(wqr[:, :, 0:16], UQ[:, h * 96 + 80:h * 96 + 96], 2, 16, wkey, neg=True)
            load_w(wqr[:, :, 16:32], UQ[:, h * 96 + 64:h * 96 + 80], 2, 16, wkey)
            load_w(wk[:, 64:128].rearrange("p (c n) -> p c n", c=1), UKV[:, h * 128:h * 128 + 64], 1, 64, wkey)
            load_w(wv.rearrange("p (c n) -> p c n", c=1), UKV[:, h * 128 + 64:h * 128 + 128], 1, 64, wkey)
            if gm == "pair":
                load_w(wg, W[:, 1960 + h * 64:1960 + (h + 2) * 64], 8, 128, wkey)
            elif gm == "single":
                load_w(wg, W[:, 1960 + h * 64:1960 + (h + 1) * 64], 8, 64, wkey)
            return (wq, wqr, wk, wv, wg, wkey, gm)

        def mla_body(h, wts):
            wq, wqr, wk, wv, wg, wkey, gm = wts
            for b in range(NB):
                tk = load_tables(b, c_r32c, c_r32s, 32)
                bs = slice(b * 512, (b + 1) * 512)
                b0_, b1_, b2_ = (0, 1, 2) if b % 2 == 0 else (3, 6, 7)
                for c in range(2):
                    P.mm(ps[b0_], wq[:, c, :], cqnT[:, c, bs], c == 0, c == 1, [*wkey, ("lat", b)], [PS(b0_)])
                for c in range(2):
                    P.mm(ps[b1_][0:32, :], wqr[:, c, :], cqnT[:, c, bs], c == 0, c == 1, [*wkey, ("lat", b)], [PS(b1_)])
                P.mm(ps[b2_], wk, ckvnT[:, bs], True, True, [*wkey, ("lat", b)], [PS(b2_)])
                P.cp("act", QT[64:128, bs], ps[b0_][64:128, :], [], [("QT", b), PS(b0_)])
                rope_evac(QT[0:32, bs], ps[b0_], ps[b1_], b0_, b1_, tk, 32, ("QT", b))
                P.cp("act", KT[64:128, bs], ps[b2_][64:128, :], [], [("KT", b), PS(b2_)])
            proj_v(lambda c, t: ckvnT[:, t * 128:(t + 1) * 128], lambda c: wv, 1, [*wkey] + [("lat", b) for b in range(NB)])
            if gm == "shift":
                gate_shift()
            else:
                proj_gate(wg, wkey, 128 if gm == "pair" else 64)

        run_heads(mla_heads, mla_w, mla_body, lambda h: attn_causal(128, 96.0 ** -0.5, h * 64))
        wf = wbv(6144, 8, 8)
        load_w(wf, W[:, 1952:1960], 8, 8, wfk)
        selflat = sel.rearrange("p k m -> p (k m)")
        for q in range(9):
            n = 512 if q < 8 else 64 * 70 - 4096
            P.dma(tmpf[q % 2][0:8, 0:n], c_sel[:, q * 512:q * 512 + n], [], [f"tmp{q % 2}"])
            P.cp("dve", selflat[0:8, q * 512:q * 512 + n], tmpf[q % 2][0:8, 0:n], [f"tmp{q % 2}"], ["sel"])
        P.barrier(carry[0:1, 1:2])
        for b in range(NB):
            bs = slice(b * 512, (b + 1) * 512)
            for c in range(8):
                P.mm(ps[0][0:8, :], wf[:, c, :], xT[:, c, bs], c == 0, c == 7, [*wfk, ("xT", b)], [PS(0)])
            P.actv(tmpf[0][0:8, :], ps[0][0:8, :], AF.Exp, ["vec"], ["tmp0", PS(0)], bias=vec[0:8, 5:6], scale=-1.0)
            P.actv(lbuf[0:8, bs], tmpf[0][0:8, :], AF.Ln, ["tmp0"], ["lbuf"], bias=1.0)
        P.add("dve", lambda e: e.tensor_tensor_scan(out=cumf[0:8, :], data0=ones_f[0:8, 0:1].to_broadcast([8, S]),
                                                     data1=lbuf[0:8, :], initial=0.0,
                                                     op0=ALU.mult, op1=ALU.subtract), ["lbuf", "ones_f"], ["cumf"])
        P.ts("dve", csp[0:8, 0, :], cumf[0:8, :], 8.0, None, ALU.mult, None, ["cumf"], ["csp"])
        P.stt(lbuf[0:8, :], cumf[0:8, :], 8.0, csp[0:8, 0, :], ALU.mult, ALU.subtract, ["cumf", "csp"], ["lbuf"])
        P.cp("dve", csp[0:8, 1, :], lbuf[0:8, :], ["lbuf"], ["csp"])
        P.tt("dve", csp[0:8, 2, :], lbuf[0:8, :], csp[0:8, 1, :], ALU.subtract, ["lbuf", "csp"], ["csp"])
        P.barrier(carry[0:1, 1:2])
        P.memset("pool", VA, 1.0, ["VA"])
        KS = lat[:, 12288:16384]

        def fox_w(h):
            wo_ = (h % 2) * 4096
            wqk = wbv(wo_, 8, 128)
            wv = wbv(wo_ + 1024, 8, 64)
            gm = gate_mode(h, fox_heads)
            wg = wbv(wo_ + 1536, 8, 128 if gm == "pair" else 64)
            wkey = (("wb", 2 * (h % 2)), ("wb", 2 * (h % 2) + 1))
            load_w(wqk[:, :, 0:64], W[:, 416 + h * 64:416 + (h + 1) * 64], 8, 64, wkey)
            load_w(wqk[:, :, 64:128], W[:, 928 + h * 64:928 + (h + 1) * 64], 8, 64, wkey)
            load_w(wv, W[:, 1440 + h * 64:1440 + (h + 1) * 64], 8, 64, wkey)
            if gm == "pair":
                load_w(wg, W[:, 2472 + h * 64:2472 + (h + 2) * 64], 8, 128, wkey)
            elif gm == "single":
                load_w(wg, W[:, 2472 + h * 64:2472 + (h + 1) * 64], 8, 64, wkey)
            return (wqk, wv, wg, wkey, gm)

        def fox_body(h, wts):
            wqk, wv, wg, wkey, gm = wts
            for b in range(NB):
                bs = slice(b * 512, (b + 1) * 512)
                bank = 2 * (b % 2)
                for c in range(8):
                    P.mm(ps[bank], wqk[:, c, :], xT[:, c, bs], c == 0, c == 7, [*wkey, ("xT", b)], [PS(bank)])
                P.cp("act", QT[0:64, bs], ps[bank][0:64, :], [], [("QT", b), PS(bank)])
                P.cp("act", KS[64:128, bs], ps[bank][64:128, :], [], [("KS", b), PS(bank)])
                for (sbank, dst, dkey, s0) in [(bank + 1, QT, ("QT", b), 0), (6 + (b % 2), KT, ("KT", b), 4)]:
                    for kind in range(4):
                        if s0 == 0:
                            rhs = csp[0:8, kind, bs] if kind < 3 else ones8[0:8, :]
                        else:
                            rhs = ones8[0:8, :] if kind == 0 else csp[0:8, kind - 1, bs]
                        P.mm(ps[sbank][0:70, :], sel[0:8, h * 8 + s0 + kind, :], rhs, kind == 0, kind == 3,
                             ["sel", ("lat", b), "ones8"], [PS(sbank)])
                    P.cp("dve", dst[64:70, bs], ps[sbank][64:70, :], [], [dkey, PS(sbank)])
            P.dma(KT[0:64, :], KS[64:128, :], [("KS", b) for b in range(NB)], [("KT", b) for b in range(NB)])
            proj_v(lambda c, t: xT[:, c, t * 128:(t + 1) * 128], lambda c: wv[:, c, :], 8, [*wkey])
            if gm == "shift":
                gate_shift()
            else:
                proj_gate(wg, wkey, 128 if gm == "pair" else 64)

        run_heads(fox_heads, fox_w, fox_body, lambda h: attn_causal(70, 0.125, 512 + h * 64))
        if debug:
            allk = [("QT", b) for b in range(NB)] + [("KT", b) for b in range(NB)] + [("GT", b) for b in range(NB)] + ["VA"] + [("lat", b) for b in range(NB)]
            P.dma(dq, QT, allk, [])
            P.dma(dk_, KT, allk, [])
            P.dma(dg, GT, allk, [])
            P.dma(dv, VA.rearrange("p t d -> p (t d)"), allk, [])
            P.dma(dlat, lat, allk, [])
        P.barrier(carry[0:1, 1:2])
        stage_c(ewo[jl], elg[jl, :], elb[jl, :], x_src, x_dst, not last)
        P.barrier(vec[0:1, 60:61])

    def odd_layer(jl, x_src, x_dst, last):
        W = ow_in[jl]
        phase_a_init()
        P.dma(vec[64:65, 8:24], osk[jl:jl + 1, :], [], ["vec"])
        P.actv(vec[64:65, 8:24], vec[64:65, 8:24], AF.Exp, [], ["vec"])

        Q2 = lat[:, 0:4096]
        QTb = lat[:, 4096:8192]
        GTb = lat[:, 8192:12288]

        def proj_rope(wm, M, dst, dname, wkey):
            tks = {}

            def front(b):
                tks[b] = load_tables(b, c_r64c, c_r64s, 64, dup=(M == 128))
                bs = slice(b * 512, (b + 1) * 512)
                o2 = 2 * (b % 2)
                for c in range(8):
                    P.mm(ps[o2][0:M, :], wm[:, c, :], xT[:, c, bs], c == 0, c == 7, [*wkey, ("xT", b)], [PS(o2)])
                P.cp("act", tmpf[2 + (b % 2)][0:M, :], ps[o2][0:M, :], [], [f"tmp{2 + (b % 2)}", PS(o2)])

            def back(b):
                tk = tks[b]
                bs = slice(b * 512, (b + 1) * 512)
                o2 = 2 * (b % 2)
                qf = tmpf[2 + (b % 2)]
                P.mm(ps[o2 + 1][0:M, :], rot64[0:M, 0:M], qf[0:M, :], True, True, ["rot64", f"tmp{2 + (b % 2)}"], [PS(o2 + 1)])
                P.tt("dve", tmpf[0][0:M, :], qf[0:M, :], tC[tk][0:M, :], ALU.mult, [("tC", tk), f"tmp{2 + (b % 2)}"], ["tmp0"])
                P.tt("dve", tmpf[1][0:M, :], ps[o2 + 1][0:M, :], tS[tk][0:M, :], ALU.mult, [("tS", tk)], ["tmp1", PS(o2 + 1)])
                P.tt("pool", dst[0:64, bs], tmpf[0][0:64, :], tmpf[1][0:64, :], ALU.add, ["tmp0", "tmp1"], [(dname, b)])
                if M == 128:
                    P.tt("pool", Q2[64:128, bs], tmpf[0][64:128, :], tmpf[1][64:128, :], ALU.add, ["tmp0", "tmp1"], [("Q2", b)])

            front(0)
            for b in range(NB):
                if b + 1 < NB:
                    front(b + 1)
                back(b)

        wgk = (("wb", 0),)
        for g in range(2):
            wkm = wbv(0, 8, 64)
            load_w(wkm, W[:, 1024 + g * 64:1024 + (g + 1) * 64], 8, 64, wgk)
            wv = wbv(1024, 8, 64)
            load_w(wv, W[:, 1152 + g * 64:1152 + (g + 1) * 64], 8, 64, wgk)
            proj_rope(wkm, 64, KT, "KT", wgk)
            proj_v(lambda c, t: xT[:, c, t * 128:(t + 1) * 128], lambda c: wv[:, c, :], 8, [*wgk])

            def swa_w(h):
                off = 2048 + (h % 2) * 2048
                wkey = (("wb", 1 + (h % 2)),)
                gm = gate_mode(h, swa_heads)
                if gm == "shift":
                    return (None, None, wkey, gm)
                n = 128 if gm == "pair" else 64
                wqm = wbv(off, 8, n)
                wgt = wbv(off + 1024, 8, n)
                load_w(wqm, W[:, h * 64:h * 64 + n], 8, n, wkey)
                load_w(wgt, W[:, 1280 + h * 64:1280 + h * 64 + n], 8, n, wkey)
                return (wqm, wgt, wkey, gm)

            def swa_body(h, wts):
                wqm, wgt, wkey, gm = wts
                if gm != "shift":
                    n = 128 if gm == "pair" else 64
                    proj_rope(wqm, n, QT, "QT", wkey)
                    proj_gate(wgt, wkey, n)
                    if gm == "pair":
                        P.dma(QTb[0:64, :], Q2[64:128, :], [("Q2", b) for b in range(NB)], [("QTb", b) for b in range(NB)])
                        P.dma(GTb[0:64, :], GT[64:128, :], [("GT", b) for b in range(NB)], [("GTb", b) for b in range(NB)])

            def swa_attn(h):
                if gate_mode(h, swa_heads) == "shift":
                    attn_swa(h * 64, vec[64:65, 8 + h:9 + h], QTb, "QTb", GTb, "GTb")
                else:
                    attn_swa(h * 64, vec[64:65, 8 + h:9 + h])

            run_heads([g * 8 + hh for hh in range(8) if g * 8 + hh in swa_heads], swa_w, swa_body, swa_attn)
        P.barrier(carry[0:1, 1:2])
        stage_c(owo[jl], olg[jl, :], olb[jl, :], x_src, x_dst, not last)
        P.barrier(vec[0:1, 60:61])

    for layer in range(nlayers):
        last = layer == nlayers - 1
        src = x_in if layer == 0 else xres
        dst = y_out if last else xres
        if kinds[layer] == 'e':
            even_layer(layer // 2, src, dst, last)
        else:
            odd_layer(layer // 2, src, dst, last)

    with contextlib.ExitStack() as st:
        counts = P.finalize(st)
    return nc, counts


def _consts():
    c = {}
    c["c_ident"] = np.eye(128, dtype=np.float32)
    k = np.arange(128)[:, None]
    q = np.arange(512)[None, :]
    c["c_mask4"] = np.concatenate([(q >= k + 128 * r).astype(np.float32) for r in range(4)], axis=1)
    q1 = np.arange(128)[None, :]
    c["c_mswa"] = np.concatenate([(q1 >= k).astype(np.float32), (q1 < k).astype(np.float32)], axis=1)
    pos = np.arange(S, dtype=np.float32)

    def tab(d):
        inv = (np.float32(10000.0) ** (-np.arange(0, d, 2, dtype=np.float32) / np.float32(d))).astype(np.float32)
        ang = (pos[None, :] * inv[:, None]).astype(np.float32)
        cs, sn = np.cos(ang).astype(np.float32), np.sin(ang).astype(np.float32)
        return np.concatenate([cs, cs], axis=0), np.concatenate([sn, sn], axis=0)

    c["c_r32c"], c["c_r32s"] = tab(32)
    c["c_r64c"], c["c_r64s"] = tab(64)
    sel = np.zeros((8, 64, 70), np.float32)
    for h in range(8):
        for kk in range(3):
            sel[h, h * 8 + kk, 64 + kk] = 1.0
        sel[0, h * 8 + 3, 67:70] = 1.0
        sel[0, h * 8 + 4, 64:67] = 1.0
        for kk in range(3):
            sel[h, h * 8 + 5 + kk, 67 + kk] = -1.0
    c["c_sel"] = sel.reshape(8, 64 * 70)
    rot = np.zeros((128, 128), np.float32)
    for hb in (0, 64):
        for m in range(32):
            rot[hb + m + 32, hb + m] = -1.0
            rot[hb + m, hb + m + 32] = 1.0
    c["c_rot64"] = rot
    return {k_: np.ascontiguousarray(v, dtype=np.float32) for k_, v in c.items()}


_CACHE = {}


def kernel(**inputs):
    if "nc" not in _CACHE:
        _CACHE["nc"] = build(4)[0]
    nc = _CACHE["nc"]
    consts = _consts()
    x = np.asarray(inputs["x"], dtype=np.float32)
    in_maps = []
    for b in range(N_CORES):
        m = {k: np.ascontiguousarray(np.asarray(v, dtype=np.float32)) for k, v in inputs.items() if k != "x"}
        m["x"] = np.ascontiguousarray(x[b])
        m.update(consts)
        in_maps.append(m)
    res = run_bass_kernel_spmd(nc, in_maps, core_ids=list(range(N_CORES)))
    return np.stack([np.asarray(r["y"], dtype=np.float32) for r in res.results], axis=0)
```

```python
import contextlib
import numpy as np
import concourse.bass as bass
import concourse.mybir as mybir
from concourse.bass_utils import run_bass_kernel_spmd

F32 = mybir.dt.float32
BF16 = mybir.dt.bfloat16
AF = mybir.ActivationFunctionType
ALU = mybir.AluOpType

S = 4096
D = 1024
NB = 8
NT = 32
ALPHA = 8.0 ** 0.25
EVEN_IN = 2984
ODD_IN = 2304
N_CORES = 4

ENG_ATTR = {"pe": "tensor", "act": "scalar", "dve": "vector", "pool": "gpsimd", "sp": "sync"}
SEM_EPOCH = 30000
N_DMA_SEMS = 12
SAME_ENG_SYNC = ("dve", "act", "pool")


class Op:
    __slots__ = ("eng", "fn", "r", "w", "dma", "deps", "flag", "sig", "dsem", "dcnt", "bar")

    def __init__(self, eng, fn, r, w, dma, bar=False):
        self.eng = eng
        self.fn = fn
        self.r = r
        self.w = w
        self.dma = dma
        self.deps = ()
        self.flag = False
        self.sig = None
        self.dsem = None
        self.dcnt = None
        self.bar = bar


class Prog:
    def __init__(self, nc):
        self.nc = nc
        self.ops = []

    def add(self, eng, fn, r=(), w=(), dma=False, bar=False):
        self.ops.append(Op(eng, fn, tuple(r), tuple(w), dma, bar))

    def mm(self, out, lhsT, rhs, start, stop, r, w, **kw):
        self.add("pe", lambda e: e.matmul(out, lhsT=lhsT, rhs=rhs, start=start, stop=stop, **kw), r, w)

    def tr(self, out, in_, ident, r, w):
        self.add("pe", lambda e: e.transpose(out, in_, ident), r, w)

    def actv(self, out, in_, func, r, w, bias=None, scale=None):
        kw = {}
        if bias is not None:
            kw["bias"] = bias
        if scale is not None:
            kw["scale"] = scale
        self.add("act", lambda e: e.activation(out=out, in_=in_, func=func, **kw), r, w)

    def tt(self, eng, out, in0, in1, op, r, w):
        self.add(eng, lambda e: e.tensor_tensor(out=out, in0=in0, in1=in1, op=op), r, w)

    def ts(self, eng, out, in0, s1, s2, op0, op1, r, w):
        if op1 is None:
            self.add(eng, lambda e: e.tensor_scalar(out=out, in0=in0, scalar1=s1, scalar2=None, op0=op0), r, w)
        else:
            self.add(eng, lambda e: e.tensor_scalar(out=out, in0=in0, scalar1=s1, scalar2=s2, op0=op0, op1=op1), r, w)

    def stt(self, out, in0, scalar, in1, op0, op1, r, w):
        self.add("dve", lambda e: e.scalar_tensor_tensor(out=out, in0=in0, scalar=scalar, in1=in1, op0=op0, op1=op1), r, w)

    def cp(self, eng, out, in_, r, w):
        if eng == "act":
            self.add(eng, lambda e: e.copy(out=out, in_=in_), r, w)
        else:
            self.add(eng, lambda e: e.tensor_copy(out=out, in_=in_), r, w)

    def memset(self, eng, ap, val, w):
        self.add(eng, lambda e: e.memset(ap, val), (), w)

    def dma(self, out, in_, r, w, q="sp"):
        self.add(q, lambda e: e.dma_start(out=out, in_=in_), r, w, dma=True)

    def barrier(self, scratch_ap):
        self.add("dve", lambda e: e.memset(scratch_ap, 0.0), (), (), bar=True)

    def finalize(self, stack):
        nc = self.nc
        ops = self.ops
        last_w = {}
        readers = {}
        last_on_eng = {}
        dma_since_bar = []
        last_bar = None
        seen_after_bar = set()
        for i, op in enumerate(ops):
            deps = set()
            if op.bar:
                for e, j in last_on_eng.items():
                    deps.add(j)
                for j in dma_since_bar:
                    deps.add(j)
                dma_since_bar = []
                last_bar = i
                seen_after_bar = set()
            else:
                for k in op.r:
                    if k in last_w:
                        deps.add(last_w[k])
                for k in op.w:
                    if k in last_w:
                        deps.add(last_w[k])
                    for ri in readers.get(k, ()):
                        deps.add(ri)
                if last_bar is not None and op.eng not in seen_after_bar:
                    deps.add(last_bar)
                    seen_after_bar.add(op.eng)
            deps.discard(i)
            for k in op.r:
                readers.setdefault(k, []).append(i)
            for k in op.w:
                last_w[k] = i
                readers[k] = []
            if op.dma:
                dma_since_bar.append(i)
            else:
                last_on_eng[op.eng] = i
            need = []
            for d in deps:
                dop = ops[d]
                if dop.dma or op.dma or dop.eng != op.eng or op.eng in SAME_ENG_SYNC:
                    need.append(d)
                    if not dop.dma:
                        dop.flag = True
            op.deps = tuple(sorted(need))
        cnt = {}
        for op in ops:
            if op.dma:
                continue
            if op.flag:
                cnt[op.eng] = cnt.get(op.eng, 0) + 1
                op.sig = cnt[op.eng]
        engs = sorted({op.eng for op in ops})
        self.sems = {}
        for e in engs:
            n = cnt.get(e, 0) // SEM_EPOCH + 1
            self.sems[e] = [stack.enter_context(nc.semaphore(f"s_{e}_{k}")) for k in range(n)]
        self.dsems = [stack.enter_context(nc.semaphore(f"d_{k}")) for k in range(N_DMA_SEMS)]
        dtot = [0] * N_DMA_SEMS
        nd = 0
        for op in ops:
            if op.dma:
                s = nd % N_DMA_SEMS
                nd += 1
                dtot[s] += 16
                op.dsem = s
                op.dcnt = dtot[s]
        final_dma = list(dtot)
        block = stack.enter_context(nc.Block())
        by_eng = {e: [op for op in ops if op.eng == e] for e in engs}
        waited = {e: {} for e in engs}

        def emit_stream(ename, eh):
            wd = waited[ename]
            for op in by_eng[ename]:
                for d in op.deps:
                    dop = ops[d]
                    if dop.dma:
                        key = ("d", dop.dsem)
                        val = dop.dcnt
                        sem = self.dsems[dop.dsem]
                    else:
                        ep = (dop.sig - 1) // SEM_EPOCH
                        key = (dop.eng, ep)
                        val = dop.sig - ep * SEM_EPOCH
                        sem = self.sems[dop.eng][ep]
                    if wd.get(key, 0) >= val:
                        continue
                    wd[key] = val
                    eh.wait_ge(sem, val)
                if op.dma:
                    if op.dcnt > 16:
                        key = ("d", op.dsem)
                        if wd.get(key, 0) < op.dcnt - 16:
                            wd[key] = op.dcnt - 16
                            eh.wait_ge(self.dsems[op.dsem], op.dcnt - 16)
                    ins = op.fn(eh)
                    ins.then_inc(self.dsems[op.dsem], 16)
                else:
                    ins = op.fn(eh)
                    if op.flag:
                        ep = (op.sig - 1) // SEM_EPOCH
                        ins.then_inc(self.sems[op.eng][ep], 1)
            if ename == "sp":
                for s in range(N_DMA_SEMS):
                    if final_dma[s] > 0:
                        eh.wait_ge(self.dsems[s], final_dma[s])

        for ename in engs:
            deco = getattr(block, ENG_ATTR[ename])

            def body(eh, ename=ename):
                emit_stream(ename, eh)

            deco(body)
        return {e: len(by_eng[e]) for e in engs}


class Arena:
    def __init__(self, nc, nbytes):
        self.t = nc.alloc_sbuf_tensor("arena", [128, nbytes // 2], BF16)
        self.nbytes = nbytes

    def view(self, off, nbytes, dtype, pattern=None, **kw):
        assert off % 32 == 0 and off + nbytes <= self.nbytes, (off, nbytes)
        ap = self.t[:, off // 2:(off + nbytes) // 2]
        if dtype == F32:
            ap = ap.bitcast(F32)
        if pattern:
            ap = ap.rearrange(pattern, **kw)
        return ap


def build(nlayers=4, kinds='eoeo', debug=False, mla_heads=range(8), fox_heads=range(8), swa_heads=range(16)):
    nc = bass.Bass("TRN2", target_bir_lowering=False)
    dt_in = lambda n, s: nc.dram_tensor(n, s, F32, kind="ExternalInput").ap()
    x_in = dt_in("x", [S, D])
    ew_in = dt_in("even_w_in", [2, D, EVEN_IN])
    eqn = dt_in("even_q_norm", [2, 256])
    euq = dt_in("even_w_uq", [2, 256, 768])
    ekvn = dt_in("even_kv_norm", [2, 128])
    eukv = dt_in("even_w_ukv", [2, 128, 1024])
    ebf = dt_in("even_b_f", [2, 8])
    ewo = dt_in("even_w_out", [2, D, D])
    elg = dt_in("even_ln_g", [2, D])
    elb = dt_in("even_ln_b", [2, D])
    ow_in = dt_in("odd_w_in", [2, D, ODD_IN])
    osk = dt_in("odd_sinks", [2, 16])
    owo = dt_in("odd_w_out", [2, D, D])
    olg = dt_in("odd_ln_g", [2, D])
    olb = dt_in("odd_ln_b", [2, D])
    c_ident = dt_in("c_ident", [128, 128])
    c_mask4 = dt_in("c_mask4", [128, 2048])
    c_mswa = dt_in("c_mswa", [128, 256])
    c_r32c = dt_in("c_r32c", [32, S])
    c_r32s = dt_in("c_r32s", [32, S])
    c_r64c = dt_in("c_r64c", [64, S])
    c_r64s = dt_in("c_r64s", [64, S])
    c_sel = dt_in("c_sel", [8, 64 * 70])
    c_rot64 = dt_in("c_rot64", [128, 128])
    y_out = nc.dram_tensor("y", [S, D], F32, kind="ExternalOutput").ap()
    xres = nc.dram_tensor("xres", [S, D], F32).ap()
    ogT = (nc.dram_tensor("ogT", [D, S], BF16, kind="ExternalOutput") if debug else nc.dram_tensor("ogT", [D, S], BF16)).ap()

    if debug:
        dq = nc.dram_tensor("dQT", [128, S], BF16, kind="ExternalOutput").ap()
        dk_ = nc.dram_tensor("dKT", [128, S], BF16, kind="ExternalOutput").ap()
        dg = nc.dram_tensor("dGT", [128, S], BF16, kind="ExternalOutput").ap()
        dv = nc.dram_tensor("dVA", [128, 32 * 66], BF16, kind="ExternalOutput").ap()
        dlat = nc.dram_tensor("dlat", [128, 16384], BF16, kind="ExternalOutput").ap()
    P = Prog(nc)
    A = Arena(nc, 210944)
    o = 0
    xT = A.view(o, 65536, BF16, "p (c n) -> p c n", c=8); o += 65536
    ident = A.view(o, 512, F32); o += 512
    ones_f = A.view(o, 512, F32); o += 512
    mask4 = A.view(o, 4096, BF16); o += 4096
    mswa = A.view(o, 512, BF16); o += 512
    vec = A.view(o, 256, F32); o += 256
    ones_b = A.view(o, 256, BF16); o += 256
    identb = A.view(o, 256, BF16); o += 256
    rot64 = A.view(o, 512, F32); o += 512
    ones8 = A.view(o, 1024, BF16); o += 1024
    tmpf = [A.view(o + i * 2048, 2048, F32) for i in range(4)]; o += 8192
    wst = [A.view(o + i * 8192, 8192, F32, "p (c n) -> p c n", c=8) for i in range(2)]; o += 16384
    wb = A.view(o, 16384, BF16); o += 16384
    tC = [A.view(o + i * 2048, 2048, F32) for i in range(2)]; o += 4096
    tS = [A.view(o + i * 2048, 2048, F32) for i in range(2)]; o += 4096
    PH = o
    o = PH
    lat = A.view(o, 32768, BF16); o += 32768
    cqnT = lat[:, 0:8192].rearrange("p (c n) -> p c n", c=2)
    ckvnT = lat[:, 8192:12288]
    kpeT = lat[:, 12288:16384]
    csp = lat[:, 0:12288].rearrange("p (k n) -> p k n", k=3)
    lbuf = A.view(o, 16384, F32)
    cumf = A.view(o + 16384, 16384, F32)
    QT = A.view(o, 8192, BF16); o += 8192
    KT = A.view(o, 8192, BF16); o += 8192
    GT = A.view(o, 8192, BF16); o += 8192
    VA = A.view(o, 4224, BF16, "p (t d) -> p t d", t=32); o += 4224
    PT = [A.view(o + i * 1024, 1024, BF16) for i in range(4)]; o += 4096
    cqf = A.view(o, 4096, F32, "p (c n) -> p c n", c=2); o += 4096
    rr2 = [cqf[:, 0, :], cqf[:, 1, :]]
    rhi2 = [A.view(o + i * 1024, 1024, BF16) for i in range(2)]; o += 2048
    rlo2 = [A.view(o + i * 1024, 1024, BF16) for i in range(2)]; o += 2048
    ogb = [A.view(o + i * 1024, 1024, BF16) for i in range(2)]; o += 2048
    sel = A.view(o, 8960, BF16, "p (k m) -> p k m", k=64); o += 8960
    carry = A.view(o, 32, F32); o += 32
    assert o <= 210944, o
    o = PH
    wo = A.view(o, 16384, BF16, "p (c n) -> p c n", c=8); o += 16384
    gt = A.view(o, 4096, F32); o += 4096
    bt = A.view(o, 4096, F32); o += 4096
    ogt = [A.view(o + i * 8192, 8192, BF16, "p (c n) -> p c n", c=8) for i in range(2)]; o += 16384
    xin = [A.view(o + i * 4096, 4096, F32) for i in range(2)]; o += 8192
    zt2 = [A.view(o + i * 4096, 4096, F32) for i in range(2)]; o += 8192
    xn2 = [A.view(o + i * 4096, 4096, F32) for i in range(2)]; o += 8192
    xo = [A.view(o + i * 4096, 4096, F32) for i in range(3)]; o += 12288
    bst2 = [A.view(o + i * 64, 64, F32) for i in range(2)]; o += 128
    mv2 = [A.view(o + i * 32, 32, F32) for i in range(2)]; o += 64
    sd2 = [A.view(o + i * 32, 32, F32) for i in range(2)]; o += 64
    assert o <= 210944, o
    ps = [nc.alloc_psum_tensor(f"ps{i}", [128, 512], F32)[:, :] for i in range(8)]
    PS = lambda i: ("ps", i)

    wst_i = [0]

    def load_w(dst, src2d, C, n, dkey, neg=False):
        k = wst_i[0] % 2
        wst_i[0] += 1
        st = wst[k][:, 0:C, 0:n]
        P.dma(st, src2d.rearrange("(c p) n -> p c n", p=128), [], [("wst", k)])
        if neg:
            P.ts("pool", dst, st, -1.0, None, ALU.mult, None, [("wst", k)], list(dkey))
        else:
            P.cp("pool", dst, st, [("wst", k)], list(dkey))

    def wbv(off, C, n):
        return wb[:, off:off + C * n].rearrange("p (c n) -> p c n", c=C)

    P.dma(ident, c_ident, [], ["ident"])
    P.cp("dve", identb, ident, ["ident"], ["identb"])
    P.dma(rot64, c_rot64, [], ["rot64"])
    P.memset("dve", ones_f, 1.0, ["ones_f"])
    P.memset("dve", ones_b, 1.0, ["ones_b"])
    P.memset("dve", ones8, 1.0, ["ones8"])
    P.memset("dve", vec, 0.0, ["vec"])
    P.memset("dve", vec[:, 0:1], 1e-6, ["vec"])
    P.memset("dve", vec[:, 1:2], 1e-5, ["vec"])
    for i in range(4):
        P.dma(tmpf[0], c_mask4[:, i * 512:(i + 1) * 512], [], ["tmp0"])
        P.ts("dve", mask4[:, i * 512:(i + 1) * 512], tmpf[0], -1.0, 30000.0, ALU.add, ALU.mult, ["tmp0"], ["mask4"])
    P.dma(tmpf[1][:, 0:256], c_mswa, [], ["tmp1"])
    P.ts("dve", mswa, tmpf[1][:, 0:256], -1.0, 30000.0, ALU.add, ALU.mult, ["tmp1"], ["mswa"])

    def transpose_tile(src_ap, skey, t, evac=("act", "dve")):
        b0 = 2 + (t % 2) * 4
        for half in range(2):
            bank = b0 + half
            for cc in range(4):
                c = half * 4 + cc
                P.tr(ps[bank][:, cc * 128:(cc + 1) * 128], src_ap[:, c * 128:(c + 1) * 128], ident, [skey, "ident"], [PS(bank)])
            P.cp(evac[half], xT[:, half * 4:half * 4 + 4, t * 128:(t + 1) * 128],
                 ps[bank].rearrange("p (c n) -> p c n", c=4), [], [("xT", t // 4), PS(bank)])

    pbufs = [(xin[0], ("xin", 0)), (xin[1], ("xin", 1)), (xo[0], ("xo", 0)), (xo[1], ("xo", 1)), (xo[2], ("xo", 2)),
             (zt2[0], ("zt", 0))]
    NPB = len(pbufs)
    for t in range(min(NPB - 1, NT)):
        P.dma(pbufs[t % NPB][0], x_in[t * 128:(t + 1) * 128, :], [], [pbufs[t % NPB][1]])
    for t in range(NT):
        tn = t + NPB - 1
        if tn < NT:
            P.dma(pbufs[tn % NPB][0], x_in[tn * 128:(tn + 1) * 128, :], [], [pbufs[tn % NPB][1]])
        transpose_tile(pbufs[t % NPB][0], pbufs[t % NPB][1], t)

    P.barrier(vec[0:1, 60:61])

    def phase_a_init():
        P.memset("pool", QT, 0.0, [("QT", b) for b in range(NB)])
        P.memset("pool", KT, 0.0, [("KT", b) for b in range(NB)])
        P.memset("pool", VA, 1.0, ["VA"])

    def load_tables(b, cs, ss, nrow, dup=False):
        k = b % 2
        P.dma(tC[k][0:nrow, :], cs[:, b * 512:(b + 1) * 512], [], [("tC", k)])
        P.dma(tS[k][0:nrow, :], ss[:, b * 512:(b + 1) * 512], [], [("tS", k)])
        if dup:
            P.dma(tC[k][64:64 + nrow, :], cs[:, b * 512:(b + 1) * 512], [], [("tC", k)])
            P.dma(tS[k][64:64 + nrow, :], ss[:, b * 512:(b + 1) * 512], [], [("tS", k)])
        return k

    def rope_evac(dst, pm, pr, bm, br, k, nrow, dkey):
        P.tt("dve", tmpf[0][0:nrow, :], pm[0:nrow, :], tC[k][0:nrow, :], ALU.mult, [("tC", k)], ["tmp0", PS(bm)])
        P.tt("dve", tmpf[1][0:nrow, :], pr[0:nrow, :], tS[k][0:nrow, :], ALU.mult, [("tS", k)], ["tmp1", PS(br)])
        P.tt("pool", dst, tmpf[0][0:nrow, :], tmpf[1][0:nrow, :], ALU.add, ["tmp0", "tmp1"], [dkey])

    def proj_v(lhs_fn, rhs_fn, nk, rkeys):
        for g4 in range(4):
            bank = 6 + (g4 % 2)
            for tt_ in range(8):
                t = g4 * 8 + tt_
                for c in range(nk):
                    P.mm(ps[bank][:, tt_ * 64:(tt_ + 1) * 64], lhs_fn(c, t), rhs_fn(c), c == 0, c == nk - 1,
                         rkeys + [("xT", t // 4)], [PS(bank)], skip_group_check=True)
            P.cp("act" if g4 % 2 == 0 else "dve", VA[:, g4 * 8:(g4 + 1) * 8, 0:64],
                 ps[bank].rearrange("p (t d) -> p t d", t=8), [], ["VA", PS(bank)])

    def proj_gate(wg, wkey, M=64):
        for b in range(NB):
            bank = 6 + (b % 2)
            for c in range(8):
                P.mm(ps[bank][0:M, :], wg[:, c, :], xT[:, c, b * 512:(b + 1) * 512], c == 0, c == 7,
                     [*wkey, ("xT", b)], [PS(bank)])
            P.actv(GT[0:M, b * 512:(b + 1) * 512], ps[bank][0:M, :], AF.Silu, [], [("GT", b), PS(bank)])

    def gate_shift():
        allg = [("GT", b) for b in range(NB)]
        P.dma(GT[0:64, :], GT[64:128, :], allg, allg)

    def gate_mode(h, heads):
        heads = list(heads)
        if h % 2 == 0 and (h + 1) in heads:
            return "pair"
        if h % 2 == 1 and (h - 1) in heads:
            return "shift"
        return "single"

    def norm_prep(obank, j, sink_ap=None):
        ob = ps[obank]
        k = j % 2
        rr, rhi, rlo = rr2[k], rhi2[k], rlo2[k]
        ck = ("cqf", k)
        if sink_ap is not None:
            P.actv(rr[64:65, :], ob[64:65, :], AF.Ln, ["vec"], [ck, PS(obank)], bias=sink_ap)
        else:
            P.actv(rr[64:65, :], ob[64:65, :], AF.Ln, [], [ck, PS(obank)])
        P.actv(rr[64:65, :], rr[64:65, :], AF.Exp, [], [ck], scale=-1.0)
        P.cp("dve", rhi[64:65, :], rr[64:65, :], [ck], [("rhi", k)])
        P.tt("dve", rlo[64:65, :], rr[64:65, :], rhi[64:65, :], ALU.subtract, [ck, ("rhi", k)], [("rlo", k)])

    def norm_fin(obank, j, hrow, GTx=None, gname="GT"):
        ob = ps[obank]
        k = j % 2
        rhi, rlo = rhi2[k], rlo2[k]
        P.mm(ps[7][0:64, :], ones_b[64:65, 0:64], rhi[64:65, :], True, False, ["ones_b", ("rhi", k)], [PS(7)])
        P.mm(ps[7][0:64, :], ones_b[64:65, 0:64], rlo[64:65, :], False, True, ["ones_b", ("rlo", k)], [PS(7)])
        P.cp("act", tmpf[2][0:64, :], ps[7][0:64, :], [], ["tmp2", PS(7)])
        P.tt("dve", tmpf[3][0:64, :], ob[0:64, :], tmpf[2][0:64, :], ALU.mult, ["tmp2"], ["tmp3", PS(obank)])
        GTx = GT if GTx is None else GTx
        P.tt("pool", ogb[k][0:64, :], tmpf[3][0:64, :], GTx[0:64, j * 512:(j + 1) * 512], ALU.mult,
             ["tmp3", (gname, j)], [("ogb", k)])
        P.dma(ogT[hrow:hrow + 64, j * 512:(j + 1) * 512], ogb[k][0:64, :], [("ogb", k)], [("ogT", j)])

    def attn_causal(dk, scale, hrow):
        pairs = [(j, kb) for j in range(NB) for kb in range(4 * j + 4)]

        def c0_of(j, kb):
            return max(0, kb - 4 * j) * 128

        def qk_sm(i):
            j, kb = pairs[i]
            sb = i % 4
            diag = kb >= 4 * j
            c0 = c0_of(j, kb)
            P.mm(ps[sb][:, c0:512], KT[0:dk, kb * 128:(kb + 1) * 128], QT[0:dk, j * 512 + c0:(j + 1) * 512], True, not diag,
                 [("KT", kb // 4), ("QT", j)], [PS(sb)])
            if diag:
                r = kb - 4 * j
                P.mm(ps[sb][:, c0:512], identb, mask4[:, r * 512 + c0:(r + 1) * 512], False, True, ["identb", "mask4"], [PS(sb)])
            P.actv(PT[sb][:, c0:512], ps[sb][:, c0:512], AF.Exp, [], [("PT", sb), PS(sb)], scale=scale)

        def pv(i):
            j, kb = pairs[i]
            sb = i % 4
            obank = 4 + (j % 2)
            c0 = c0_of(j, kb)
            P.mm(ps[obank][0:65, c0:512], VA[:, kb, 0:65], PT[sb][:, c0:512], kb == 0, kb == 4 * j + 3,
                 ["VA", ("PT", sb)], [PS(obank)], skip_group_check=True)

        n = len(pairs)
        qk_sm(0)
        qk_sm(1)
        qk_sm(2)
        for i in range(n):
            if i + 3 < n:
                qk_sm(i + 3)
            pv(i)
            j, kb = pairs[i]
            if kb == 4 * j + 3:
                if j > 0:
                    norm_fin(4 + ((j - 1) % 2), j - 1, hrow)
                norm_prep(4 + (j % 2), j)
        norm_fin(4 + ((NB - 1) % 2), NB - 1, hrow)

    def attn_swa(hrow, sink_ap, QTx=None, qname="QT", GTx=None, gname="GT"):
        QTx = QT if QTx is None else QTx
        def obank_of(qb):
            return 4 + ((qb // 4) % 2)

        def qk(kb):
            n = 256 if kb < NT - 1 else 128
            sb = kb % 4
            P.mm(ps[sb][:, 0:n], KT[0:64, kb * 128:(kb + 1) * 128], QTx[0:64, kb * 128:kb * 128 + n], True, False,
                 [("KT", kb // 4), (qname, kb // 4), (qname, min((kb + 1) // 4, NB - 1))], [PS(sb)])
            P.mm(ps[sb][:, 0:n], identb, mswa[:, 0:n], False, True, ["identb", "mswa"], [PS(sb)])
            P.actv(PT[sb][:, 0:n], ps[sb][:, 0:n], AF.Exp, [], [("PT", sb), PS(sb)], scale=0.125)

        def pv(kb):
            sb = kb % 4
            ob = obank_of(kb)
            P.mm(ps[ob][0:65, (kb % 4) * 128:(kb % 4) * 128 + 128], VA[:, kb, 0:65], PT[sb][:, 0:128], kb == 0, True,
                 ["VA", ("PT", sb)], [PS(ob)], skip_group_check=True)
            if kb < NT - 1:
                ob2 = obank_of(kb + 1)
                c0 = ((kb + 1) % 4) * 128
                P.mm(ps[ob2][0:65, c0:c0 + 128], VA[:, kb, 0:65], PT[sb][:, 128:256], True, False,
                     ["VA", ("PT", sb)], [PS(ob2)], skip_group_check=True)
            if kb % 4 == 2 and kb // 4 > 0:
                norm_fin(obank_of(kb - 4), kb // 4 - 1, hrow, GTx, gname)
            if kb % 4 == 3:
                norm_prep(ob, kb // 4, sink_ap)

        qk(0)
        qk(1)
        qk(2)
        for kb in range(NT):
            if kb + 3 < NT:
                qk(kb + 3)
            pv(kb)
        norm_fin(obank_of(NT - 1), NB - 1, hrow, GTx, gname)

    def stage_c(wo_src, g_src, b_src, x_src, x_dst, do_transpose):
        for q4 in range(4):
            load_w(wo[:, :, q4 * 256:(q4 + 1) * 256], wo_src[:, q4 * 256:(q4 + 1) * 256], 8, 256, ("wo",))
        P.dma(gt, g_src.partition_broadcast(128), [], ["gt"])
        P.dma(bt, b_src.partition_broadcast(128), [], ["bt"])
        ogv = ogT.rearrange("(c p) n -> p c n", p=128)

        def ld_og(g):
            P.dma(ogt[g % 2], ogv[:, :, g * 512:(g + 1) * 512], [("ogT", g)], [("ogt", g % 2)])

        def ld_x(t):
            P.dma(xin[t % 2], x_src[t * 128:(t + 1) * 128, :], [("xres", t)], [("xin", t % 2)])

        def finish(t):
            k3 = t % 3
            P.dma(x_dst[t * 128:(t + 1) * 128, :], xo[k3], [("xo", k3)], [("xres", t)], q="act")
            if do_transpose:
                transpose_tile(xo[k3], ("xo", k3), t, evac=("act", "act"))

        ld_og(0)
        ld_x(0)
        for t in range(NT):
            k = t % 2
            g, tl = t // 4, t % 4
            zt, xn, bst, mv, sd = zt2[k], xn2[k], bst2[k], mv2[k], sd2[k]
            kz, kx, kb_, km, ks = ("zt", k), ("xn", k), ("bst", k), ("mv", k), ("sd", k)
            if tl == 0 and g + 1 < NB:
                ld_og(g + 1)
            if t + 1 < NT:
                ld_x(t + 1)
            pb = (t % 2) * 4
            for half in range(2):
                for c in range(8):
                    P.mm(ps[pb + half], ogt[g % 2][:, c, tl * 128:(tl + 1) * 128], wo[:, c, half * 512:(half + 1) * 512], c == 0, c == 7,
                         [("ogt", g % 2), "wo"], [PS(pb + half)])
                P.stt(zt[:, half * 512:(half + 1) * 512], xin[k][:, half * 512:(half + 1) * 512], ALPHA, ps[pb + half],
                      ALU.mult, ALU.add, [("xin", k)], [kz, PS(pb + half)])
                P.add("dve", lambda e, half=half, bst=bst, zt=zt: e.bn_stats(out=bst[:, half * 6:(half + 1) * 6], in_=zt[:, half * 512:(half + 1) * 512]),
                      [kz], [kb_])
            P.add("dve", lambda e, bst=bst, mv=mv: e.bn_aggr(out=mv[:, 0:2], in_=bst[:, 0:12]), [kb_], [km])
            P.actv(sd[:, 0:1], mv[:, 1:2], AF.Sqrt, [km, "vec"], [ks], bias=vec[:, 1:2], scale=1.0)
            P.add("dve", lambda e, sd=sd: e.reciprocal(out=sd[:, 1:2], in_=sd[:, 0:1]), [], [ks])
            P.ts("dve", sd[:, 2:3], mv[:, 0:1], -1.0, sd[:, 1:2], ALU.mult, ALU.mult, [km], [ks])
            P.actv(xn, zt, AF.Identity, [kz, ks], [kx], bias=sd[:, 2:3], scale=sd[:, 1:2])
            if t >= 2:
                finish(t - 2)
            P.tt("pool", xn, xn, gt, ALU.mult, ["gt"], [kx])
            P.tt("pool", xo[t % 3], xn, bt, ALU.add, [kx, "bt"], [("xo", t % 3)])
        finish(NT - 2)
        finish(NT - 1)

    def run_heads(heads, wfn, projfn, attnfn):
        heads = list(heads)
        if not heads:
            return
        cur = wfn(heads[0])
        for i, h in enumerate(heads):
            projfn(h, cur)
            nxt = wfn(heads[i + 1]) if i + 1 < len(heads) else None
            attnfn(h)
            cur = nxt

    def even_layer(jl, x_src, x_dst, last):
        W = ew_in[jl]
        phase_a_init()
        wL = (("wb", 0), ("wb", 1))
        wfk = (("wb", 3),)
        wq0 = wbv(0, 8, 256)
        wkv0 = wbv(2048, 8, 128)
        wkp = wbv(3072, 8, 32)
        wkr = wbv(3328, 8, 32)
        load_w(wq0, W[:, 0:256], 8, 256, wL)
        load_w(wkv0, W[:, 256:384], 8, 128, wL)
        load_w(wkp, W[:, 384:416], 8, 32, wL)
        load_w(wkr[:, :, 0:16], W[:, 400:416], 8, 16, wL, neg=True)
        load_w(wkr[:, :, 16:32], W[:, 384:400], 8, 16, wL)
        for c in range(2):
            P.dma(vec[:, 2 + c:3 + c], eqn[jl, c * 128:(c + 1) * 128].rearrange("(p o) -> p o", o=1), [], ["vec"])
        P.dma(vec[:, 4:5], ekvn[jl, :].rearrange("(p o) -> p o", o=1), [], ["vec"])
        P.dma(vec[0:8, 5:6], ebf[jl, :].rearrange("(p o) -> p o", o=1), [], ["vec"])
        P.ts("pool", vec[0:8, 5:6], vec[0:8, 5:6], -1.0, None, ALU.mult, None, [], ["vec"])
        for b in range(NB):
            tk = load_tables(b, c_r32c, c_r32s, 32)
            xs = lambda c: xT[:, c, b * 512:(b + 1) * 512]
            for g, (wv_, bank) in enumerate([(wq0[:, :, 0:128], 0), (wq0[:, :, 128:256], 1), (wkv0, 2)]):
                for c in range(8):
                    P.mm(ps[bank], wv_[:, c, :], xs(c), c == 0, c == 7, [*wL, ("xT", b)], [PS(bank)])
            for (wv_, bank) in [(wkp, 3), (wkr, 6)]:
                for c in range(8):
                    P.mm(ps[bank][0:32, :], wv_[:, c, :], xs(c), c == 0, c == 7, [*wL, ("xT", b)], [PS(bank)])
            for c in range(2):
                P.cp("dve", cqf[:, c, :], ps[c], [], [("cqf", c), PS(c)])
                P.actv(tmpf[2 + c], cqf[:, c, :], AF.Square, [("cqf", c)], [f"tmp{2 + c}"])
            P.tt("dve", tmpf[2], tmpf[2], tmpf[3], ALU.add, ["tmp3"], ["tmp2"])
            P.mm(ps[7], ones_f, tmpf[2], True, True, ["ones_f", "tmp2"], [PS(7)])
            P.actv(tmpf[2], ps[7], AF.Sqrt, ["vec"], ["tmp2", PS(7)], bias=vec[:, 0:1], scale=1.0 / 256)
            P.add("dve", lambda e: e.reciprocal(out=tmpf[3], in_=tmpf[2]), ["tmp2"], ["tmp3"])
            for c in range(2):
                P.stt(cqnT[:, c, b * 512:(b + 1) * 512], cqf[:, c, :], vec[:, 2 + c:3 + c], tmpf[3], ALU.mult, ALU.mult,
                      [("cqf", c), "tmp3", "vec"], [("lat", b)])
            P.cp("dve", cqf[:, 0, :], ps[2], [], [("cqf", 0), PS(2)])
            P.actv(tmpf[2], cqf[:, 0, :], AF.Square, [("cqf", 0)], ["tmp2"])
            P.mm(ps[7], ones_f, tmpf[2], True, True, ["ones_f", "tmp2"], [PS(7)])
            P.actv(tmpf[2], ps[7], AF.Sqrt, ["vec"], ["tmp2", PS(7)], bias=vec[:, 0:1], scale=1.0 / 128)
            P.add("dve", lambda e: e.reciprocal(out=tmpf[3], in_=tmpf[2]), ["tmp2"], ["tmp3"])
            P.stt(ckvnT[:, b * 512:(b + 1) * 512], cqf[:, 0, :], vec[:, 4:5], tmpf[3], ALU.mult, ALU.mult,
                  [("cqf", 0), "tmp3", "vec"], [("lat", b)])
            rope_evac(kpeT[0:32, b * 512:(b + 1) * 512], ps[3], ps[6], 3, 6, tk, 32, ("lat", b))
        UQ = euq[jl]
        UKV = eukv[jl]
        for b in range(NB):
            P.cp("pool", KT[0:32, b * 512:(b + 1) * 512], kpeT[0:32, b * 512:(b + 1) * 512], [("lat", b)], [("KT", b)])

        def mla_w(h):
            wo_ = (h % 2) * 4096
            wq = wbv(wo_, 2, 128)
            wqr = wbv(wo_ + 256, 2, 32)
            wk = wb[:, wo_ + 320:wo_ + 448]
            wv = wb[:, wo_ + 448:wo_ + 512]
            gm = gate_mode(h, mla_heads)
            wg = wbv(wo_ + 512, 8, 128 if gm == "pair" else 64)
            wkey = (("wb", 2 * (h % 2)), ("wb", 2 * (h % 2) + 1))
            P.memset("pool", wq[:, :, 32:64], 0.0, list(wkey))
            P.memset("pool", wk[:, 0:64], 0.0, [*wkey])
            load_w(wq[:, :, 0:32], UQ[:, h * 96 + 64:h * 96 + 96], 2, 32, wkey)
            load_w(wq[:, :, 64:128], UQ[:, h * 96:h * 96 + 64], 2, 64, wkey)
            load_w(wqr[:, :, 0:16], UQ[:, h * 96 + 80:h * 96 + 96], 2, 16, wkey, neg=True)
            load_w(wqr[:, :, 16:32], UQ[:, h * 96 + 64:h * 96 + 80], 2, 16, wkey)
            load_w(wk[:, 64:128].rearrange("p (c n) -> p c n", c=1), UKV[:, h * 128:h * 128 + 64], 1, 64, wkey)
            load_w(wv.rearrange("p (c n) -> p c n", c=1), UKV[:, h * 128 + 64:h * 128 + 128], 1, 64, wkey)
            if gm == "pair":
                load_w(wg, W[:, 1960 + h * 64:1960 + (h + 2) * 64], 8, 128, wkey)
            elif gm == "single":
                load_w(wg, W[:, 1960 + h * 64:1960 + (h + 1) * 64], 8, 64, wkey)
            return (wq, wqr, wk, wv, wg, wkey, gm)

        def mla_body(h, wts):
            wq, wqr, wk, wv, wg, wkey, gm = wts
            for b in range(NB):
                tk = load_tables(b, c_r32c, c_r32s, 32)
                bs = slice(b * 512, (b + 1) * 512)
                b0_, b1_, b2_ = (0, 1, 2) if b % 2 == 0 else (3, 6, 7)
                for c in range(2):
                    P.mm(ps[b0_], wq[:, c, :], cqnT[:, c, bs], c == 0, c == 1, [*wkey, ("lat", b)], [PS(b0_)])
                for c in range(2):
                    P.mm(ps[b1_][0:32, :], wqr[:, c, :], cqnT[:, c, bs], c == 0, c == 1, [*wkey, ("lat", b)], [PS(b1_)])
                P.mm(ps[b2_], wk, ckvnT[:, bs], True, True, [*wkey, ("lat", b)], [PS(b2_)])
                P.cp("act", QT[64:128, bs], ps[b0_][64:128, :], [], [("QT", b), PS(b0_)])
                rope_evac(QT[0:32, bs], ps[b0_], ps[b1_], b0_, b1_, tk, 32, ("QT", b))
                P.cp("act", KT[64:128, bs], ps[b2_][64:128, :], [], [("KT", b), PS(b2_)])
            proj_v(lambda c, t: ckvnT[:, t * 128:(t + 1) * 128], lambda c: wv, 1, [*wkey] + [("lat", b) for b in range(NB)])
            if gm == "shift":
                gate_shift()
            else:
                proj_gate(wg, wkey, 128 if gm == "pair" else 64)

        run_heads(mla_heads, mla_w, mla_body, lambda h: attn_causal(128, 96.0 ** -0.5, h * 64))
        wf = wbv(6144, 8, 8)
        load_w(wf, W[:, 1952:1960], 8, 8, wfk)
        selflat = sel.rearrange("p k m -> p (k m)")
        for q in range(9):
            n = 512 if q < 8 else 64 * 70 - 4096
            P.dma(tmpf[q % 2][0:8, 0:n], c_sel[:, q * 512:q * 512 + n], [], [f"tmp{q % 2}"])
            P.cp("dve", selflat[0:8, q * 512:q * 512 + n], tmpf[q % 2][0:8, 0:n], [f"tmp{q % 2}"], ["sel"])
        P.barrier(carry[0:1, 1:2])
        for b in range(NB):
            bs = slice(b * 512, (b + 1) * 512)
            for c in range(8):
                P.mm(ps[0][0:8, :], wf[:, c, :], xT[:, c, bs], c == 0, c == 7, [*wfk, ("xT", b)], [PS(0)])
            P.actv(tmpf[0][0:8, :], ps[0][0:8, :], AF.Exp, ["vec"], ["tmp0", PS(0)], bias=vec[0:8, 5:6], scale=-1.0)
            P.actv(lbuf[0:8, bs], tmpf[0][0:8, :], AF.Ln, ["tmp0"], ["lbuf"], bias=1.0)
        P.add("dve", lambda e: e.tensor_tensor_scan(out=cumf[0:8, :], data0=ones_f[0:8, 0:1].to_broadcast([8, S]),
                                                     data1=lbuf[0:8, :], initial=0.0,
                                                     op0=ALU.mult, op1=ALU.subtract), ["lbuf", "ones_f"], ["cumf"])
        P.ts("dve", csp[0:8, 0, :], cumf[0:8, :], 8.0, None, ALU.mult, None, ["cumf"], ["csp"])
        P.stt(lbuf[0:8, :], cumf[0:8, :], 8.0, csp[0:8, 0, :], ALU.mult, ALU.subtract, ["cumf", "csp"], ["lbuf"])
        P.cp("dve", csp[0:8, 1, :], lbuf[0:8, :], ["lbuf"], ["csp"])
        P.tt("dve", csp[0:8, 2, :], lbuf[0:8, :], csp[0:8, 1, :], ALU.subtract, ["lbuf", "csp"], ["csp"])
        P.barrier(carry[0:1, 1:2])
        P.memset("pool", VA, 1.0, ["VA"])
        KS = lat[:, 12288:16384]

        def fox_w(h):
            wo_ = (h % 2) * 4096
            wqk = wbv(wo_, 8, 128)
            wv = wbv(wo_ + 1024, 8, 64)
            gm = gate_mode(h, fox_heads)
            wg = wbv(wo_ + 1536, 8, 128 if gm == "pair" else 64)
            wkey = (("wb", 2 * (h % 2)), ("wb", 2 * (h % 2) + 1))
            load_w(wqk[:, :, 0:64], W[:, 416 + h * 64:416 + (h + 1) * 64], 8, 64, wkey)
            load_w(wqk[:, :, 64:128], W[:, 928 + h * 64:928 + (h + 1) * 64], 8, 64, wkey)
            load_w(wv, W[:, 1440 + h * 64:1440 + (h + 1) * 64], 8, 64, wkey)
            if gm == "pair":
                load_w(wg, W[:, 2472 + h * 64:2472 + (h + 2) * 64], 8, 128, wkey)
            elif gm == "single":
                load_w(wg, W[:, 2472 + h * 64:2472 + (h + 1) * 64], 8, 64, wkey)
            return (wqk, wv, wg, wkey, gm)

        def fox_body(h, wts):
            wqk, wv, wg, wkey, gm = wts
            for b in range(NB):
                bs = slice(b * 512, (b + 1) * 512)
                bank = 2 * (b % 2)
                for c in range(8):
                    P.mm(ps[bank], wqk[:, c, :], xT[:, c, bs], c == 0, c == 7, [*wkey, ("xT", b)], [PS(bank)])
                P.cp("act", QT[0:64, bs], ps[bank][0:64, :], [], [("QT", b), PS(bank)])
                P.cp("act", KS[64:128, bs], ps[bank][64:128, :], [], [("KS", b), PS(bank)])
                for (sbank, dst, dkey, s0) in [(bank + 1, QT, ("QT", b), 0), (6 + (b % 2), KT, ("KT", b), 4)]:
                    for kind in range(4):
                        if s0 == 0:
                            rhs = csp[0:8, kind, bs] if kind < 3 else ones8[0:8, :]
                        else:
                            rhs = ones8[0:8, :] if kind == 0 else csp[0:8, kind - 1, bs]
                        P.mm(ps[sbank][0:70, :], sel[0:8, h * 8 + s0 + kind, :], rhs, kind == 0, kind == 3,
                             ["sel", ("lat", b), "ones8"], [PS(sbank)])
                    P.cp("dve", dst[64:70, bs], ps[sbank][64:70, :], [], [dkey, PS(sbank)])
            P.dma(KT[0:64, :], KS[64:128, :], [("KS", b) for b in range(NB)], [("KT", b) for b in range(NB)])
            proj_v(lambda c, t: xT[:, c, t * 128:(t + 1) * 128], lambda c: wv[:, c, :], 8, [*wkey])
            if gm == "shift":
                gate_shift()
            else:
                proj_gate(wg, wkey, 128 if gm == "pair" else 64)

        run_heads(fox_heads, fox_w, fox_body, lambda h: attn_causal(70, 0.125, 512 + h * 64))
        if debug:
            allk = [("QT", b) for b in range(NB)] + [("KT", b) for b in range(NB)] + [("GT", b) for b in range(NB)] + ["VA"] + [("lat", b) for b in range(NB)]
            P.dma(dq, QT, allk, [])
            P.dma(dk_, KT, allk, [])
            P.dma(dg, GT, allk, [])
            P.dma(dv, VA.rearrange("p t d -> p (t d)"), allk, [])
            P.dma(dlat, lat, allk, [])
        P.barrier(carry[0:1, 1:2])
        stage_c(ewo[jl], elg[jl, :], elb[jl, :], x_src, x_dst, not last)
        P.barrier(vec[0:1, 60:61])

    def odd_layer(jl, x_src, x_dst, last):
        W = ow_in[jl]
        phase_a_init()
        P.dma(vec[64:65, 8:24], osk[jl:jl + 1, :], [], ["vec"])
        P.actv(vec[64:65, 8:24], vec[64:65, 8:24], AF.Exp, [], ["vec"])

        Q2 = lat[:, 0:4096]
        QTb = lat[:, 4096:8192]
        GTb = lat[:, 8192:12288]

        def proj_rope(wm, M, dst, dname, wkey):
            tks = {}

            def front(b):
                tks[b] = load_tables(b, c_r64c, c_r64s, 64, dup=(M == 128))
                bs = slice(b * 512, (b + 1) * 512)
                o2 = 2 * (b % 2)
                for c in range(8):
                    P.mm(ps[o2][0:M, :], wm[:, c, :], xT[:, c, bs], c == 0, c == 7, [*wkey, ("xT", b)], [PS(o2)])
                P.cp("act", tmpf[2 + (b % 2)][0:M, :], ps[o2][0:M, :], [], [f"tmp{2 + (b % 2)}", PS(o2)])

            def back(b):
                tk = tks[b]
                bs = slice(b * 512, (b + 1) * 512)
                o2 = 2 * (b % 2)
                qf = tmpf[2 + (b % 2)]
                P.mm(ps[o2 + 1][0:M, :], rot64[0:M, 0:M], qf[0:M, :], True, True, ["rot64", f"tmp{2 + (b % 2)}"], [PS(o2 + 1)])
                P.tt("dve", tmpf[0][0:M, :], qf[0:M, :], tC[tk][0:M, :], ALU.mult, [("tC", tk), f"tmp{2 + (b % 2)}"], ["tmp0"])
                P.tt("dve", tmpf[1][0:M, :], ps[o2 + 1][0:M, :], tS[tk][0:M, :], ALU.mult, [("tS", tk)], ["tmp1", PS(o2 + 1)])
                P.tt("pool", dst[0:64, bs], tmpf[0][0:64, :], tmpf[1][0:64, :], ALU.add, ["tmp0", "tmp1"], [(dname, b)])
                if M == 128:
                    P.tt("pool", Q2[64:128, bs], tmpf[0][64:128, :], tmpf[1][64:128, :], ALU.add, ["tmp0", "tmp1"], [("Q2", b)])

            front(0)
            for b in range(NB):
                if b + 1 < NB:
                    front(b + 1)
                back(b)

        wgk = (("wb", 0),)
        for g in range(2):
            wkm = wbv(0, 8, 64)
            load_w(wkm, W[:, 1024 + g * 64:1024 + (g + 1) * 64], 8, 64, wgk)
            wv = wbv(1024, 8, 64)
            load_w(wv, W[:, 1152 + g * 64:1152 + (g + 1) * 64], 8, 64, wgk)
            proj_rope(wkm, 64, KT, "KT", wgk)
            proj_v(lambda c, t: xT[:, c, t * 128:(t + 1) * 128], lambda c: wv[:, c, :], 8, [*wgk])

            def swa_w(h):
                off = 2048 + (h % 2) * 2048
                wkey = (("wb", 1 + (h % 2)),)
                gm = gate_mode(h, swa_heads)
                if gm == "shift":
                    return (None, None, wkey, gm)
                n = 128 if gm == "pair" else 64
                wqm = wbv(off, 8, n)
                wgt = wbv(off + 1024, 8, n)
                load_w(wqm, W[:, h * 64:h * 64 + n], 8, n, wkey)
                load_w(wgt, W[:, 1280 + h * 64:1280 + h * 64 + n], 8, n, wkey)
                return (wqm, wgt, wkey, gm)

            def swa_body(h, wts):
                wqm, wgt, wkey, gm = wts
                if gm != "shift":
                    n = 128 if gm == "pair" else 64
                    proj_rope(wqm, n, QT, "QT", wkey)
                    proj_gate(wgt, wkey, n)
                    if gm == "pair":
                        P.dma(QTb[0:64, :], Q2[64:128, :], [("Q2", b) for b in range(NB)], [("QTb", b) for b in range(NB)])
                        P.dma(GTb[0:64, :], GT[64:128, :], [("GT", b) for b in range(NB)], [("GTb", b) for b in range(NB)])

            def swa_attn(h):
                if gate_mode(h, swa_heads) == "shift":
                    attn_swa(h * 64, vec[64:65, 8 + h:9 + h], QTb, "QTb", GTb, "GTb")
                else:
                    attn_swa(h * 64, vec[64:65, 8 + h:9 + h])

            run_heads([g * 8 + hh for hh in range(8) if g * 8 + hh in swa_heads], swa_w, swa_body, swa_attn)
        P.barrier(carry[0:1, 1:2])
        stage_c(owo[jl], olg[jl, :], olb[jl, :], x_src, x_dst, not last)
        P.barrier(vec[0:1, 60:61])

    for layer in range(nlayers):
        last = layer == nlayers - 1
        src = x_in if layer == 0 else xres
        dst = y_out if last else xres
        if kinds[layer] == 'e':
            even_layer(layer // 2, src, dst, last)
        else:
            odd_layer(layer // 2, src, dst, last)

    with contextlib.ExitStack() as st:
        counts = P.finalize(st)
    return nc, counts


def _consts():
    c = {}
    c["c_ident"] = np.eye(128, dtype=np.float32)
    k = np.arange(128)[:, None]
    q = np.arange(512)[None, :]
    c["c_mask4"] = np.concatenate([(q >= k + 128 * r).astype(np.float32) for r in range(4)], axis=1)
    q1 = np.arange(128)[None, :]
    c["c_mswa"] = np.concatenate([(q1 >= k).astype(np.float32), (q1 < k).astype(np.float32)], axis=1)
    pos = np.arange(S, dtype=np.float32)

    def tab(d):
        inv = (np.float32(10000.0) ** (-np.arange(0, d, 2, dtype=np.float32) / np.float32(d))).astype(np.float32)
        ang = (pos[None, :] * inv[:, None]).astype(np.float32)
        cs, sn = np.cos(ang).astype(np.float32), np.sin(ang).astype(np.float32)
        return np.concatenate([cs, cs], axis=0), np.concatenate([sn, sn], axis=0)

    c["c_r32c"], c["c_r32s"] = tab(32)
    c["c_r64c"], c["c_r64s"] = tab(64)
    sel = np.zeros((8, 64, 70), np.float32)
    for h in range(8):
        for kk in range(3):
            sel[h, h * 8 + kk, 64 + kk] = 1.0
        sel[0, h * 8 + 3, 67:70] = 1.0
        sel[0, h * 8 + 4, 64:67] = 1.0
        for kk in range(3):
            sel[h, h * 8 + 5 + kk, 67 + kk] = -1.0
    c["c_sel"] = sel.reshape(8, 64 * 70)
    rot = np.zeros((128, 128), np.float32)
    for hb in (0, 64):
        for m in range(32):
            rot[hb + m + 32, hb + m] = -1.0
            rot[hb + m, hb + m + 32] = 1.0
    c["c_rot64"] = rot
    return {k_: np.ascontiguousarray(v, dtype=np.float32) for k_, v in c.items()}


_CACHE = {}


def kernel(**inputs):
    if "nc" not in _CACHE:
        _CACHE["nc"] = build(4)[0]
    nc = _CACHE["nc"]
    consts = _consts()
    x = np.asarray(inputs["x"], dtype=np.float32)
    in_maps = []
    for b in range(N_CORES):
        m = {k: np.ascontiguousarray(np.asarray(v, dtype=np.float32)) for k, v in inputs.items() if k != "x"}
        m["x"] = np.ascontiguousarray(x[b])
        m.update(consts)
        in_maps.append(m)
    res = run_bass_kernel_spmd(nc, in_maps, core_ids=list(range(N_CORES)))
    return np.stack([np.asarray(r["y"], dtype=np.float32) for r in res.results], axis=0)
```
